# Optimizing a Trainium2 kernel written in Bass

```python
import jax, jax.numpy as jnp
from jax import lax
import numpy as np

D_MODEL = 2048
BATCH = 8
SEQ = 4096
DEPTH = 4

MIX_WIDTH = D_MODEL
GROUP_WIDTH = MIX_WIDTH // 4
HEAD_DIM = 128
N_HEADS_PER_MIXER = GROUP_WIDTH // HEAD_DIM
CHUNK = 128
SHORT_CONV = 3
CONFORMER_CONV = 31
POOL_WINDOWS = (2, 4, 8, 16)
POOL_GROUP = GROUP_WIDTH // len(POOL_WINDOWS)
D_FF = ((8 * D_MODEL // 3 + 255) // 256) * 256
PLE_DIM = 256
EPS = 1e-6
A_COLS = 2 * GROUP_WIDTH
B_COLS = 3 * GROUP_WIDTH
C_COLS = 2 * GROUP_WIDTH
D_COLS = GROUP_WIDTH
IN_COLS = A_COLS + B_COLS + C_COLS + D_COLS

kernel_name = "hybrid_sgu_conv_conformer_pool_trunk"


def _rms(x, g):
    xf = x.astype(jnp.float32)
    y = xf * lax.rsqrt(jnp.mean(xf * xf, axis=-1, keepdims=True) + EPS)
    return (y * g.astype(jnp.float32)).astype(x.dtype)


def _ln(x, g, b):
    xf = x.astype(jnp.float32)
    mu = jnp.mean(xf, axis=-1, keepdims=True)
    xc = xf - mu
    var = jnp.mean(xc * xc, axis=-1, keepdims=True)
    y = xc * lax.rsqrt(var + EPS) * g.astype(jnp.float32) + b.astype(jnp.float32)
    return y.astype(x.dtype)


def _causal_dwconv(x, w):
    k, c = w.shape
    return lax.conv_general_dilated(
        x, w[:, None, :].astype(x.dtype), window_strides=(1,), padding=[(k - 1, 0)],
        dimension_numbers=("NWC", "WIO", "NWC"), feature_group_count=c)


def _mixer_sgu(z, ln_g, ln_b, w_s, b_s):
    bsz, s, _ = z.shape
    n = s // CHUNK
    z = jax.nn.gelu(z)
    u, v = jnp.split(z, 2, axis=-1)
    v = _ln(v.reshape(bsz, s, N_HEADS_PER_MIXER, HEAD_DIM), ln_g, ln_b)
    v = v.reshape(bsz, n, CHUNK, N_HEADS_PER_MIXER, HEAD_DIM)
    mask = jnp.tril(jnp.ones((CHUNK, CHUNK), dtype=bool))
    wm = jnp.where(mask[None], w_s, jnp.zeros((), w_s.dtype))
    sp = jnp.einsum("hts,bnshd->bnthd", wm, v) + b_s.T[:, :, None]
    out = u.reshape(bsz, n, CHUNK, N_HEADS_PER_MIXER, HEAD_DIM) * sp
    return out.reshape(bsz, s, GROUP_WIDTH)


def _mixer_shortconv(z, conv_w):
    h, bg, cg = jnp.split(z, 3, axis=-1)
    return bg * _causal_dwconv(cg * h, conv_w)


def _mixer_conformer(z, conv_w, conv_b, ln_g, ln_b):
    a, g = jnp.split(z, 2, axis=-1)
    h = a * jax.nn.sigmoid(g)
    h = _causal_dwconv(h, conv_w) + conv_b
    h = _ln(h, ln_g, ln_b)
    return jax.nn.silu(h)


def _mixer_pool(z, pool_w, pool_scale):
    bsz, s, _ = z.shape
    zg = z.reshape(bsz, s, len(POOL_WINDOWS), POOL_GROUP)
    cs = jnp.cumsum(zg.astype(jnp.float32), axis=1)
    pos = jnp.arange(1, s + 1, dtype=jnp.int32)
    pooled = []
    for gi, w in enumerate(POOL_WINDOWS):
        c = cs[:, :, gi]
        lagged = jnp.pad(c, ((0, 0), (w, 0), (0, 0)))[:, :s]
        count = jnp.minimum(pos, w).astype(jnp.float32)
        pooled.append((c - lagged) / count[None, :, None])
    pooled = jnp.stack(pooled, axis=2).astype(z.dtype) - zg
    out = jnp.einsum("bsgc,gcd->bsgd", pooled, pool_w).reshape(bsz, s, GROUP_WIDTH)
    return out * pool_scale


def setup_inputs(seed: int = 0) -> dict:
    key = jax.random.key(seed)
    ks = jax.random.split(key, 24)
    f = jnp.float32
    L, D, G, H, hd = DEPTH, D_MODEL, GROUP_WIDTH, N_HEADS_PER_MIXER, HEAD_DIM
    nrm = lambda k, shape, scale: jax.random.normal(k, shape, f) * scale
    gain = lambda k, shape: 1.0 + 0.05 * jax.random.normal(k, shape, f)
    return {
        "x": jax.random.normal(ks[0], (BATCH, SEQ, D), f),
        "p": jax.random.normal(ks[1], (DEPTH, BATCH, SEQ, PLE_DIM), f),
        "norm_mix_g": gain(ks[2], (L, D)),
        "w_in": nrm(ks[3], (L, D, IN_COLS), D ** -0.5),
        "sgu_ln_g": gain(ks[4], (L, H, hd)),
        "sgu_ln_b": nrm(ks[5], (L, H, hd), 0.02),
        "sgu_w": nrm(ks[6], (L, H, CHUNK, CHUNK), CHUNK ** -0.5),
        "sgu_b": gain(ks[7], (L, H, CHUNK)),
        "sc_conv_w": nrm(ks[8], (L, SHORT_CONV, G), SHORT_CONV ** -0.5),
        "cf_conv_w": nrm(ks[9], (L, CONFORMER_CONV, G), CONFORMER_CONV ** -0.5),
        "cf_conv_b": nrm(ks[10], (L, G), 0.02),
        "cf_ln_g": gain(ks[11], (L, G)),
        "cf_ln_b": nrm(ks[12], (L, G), 0.02),
        "pool_w": nrm(ks[13], (L, len(POOL_WINDOWS), POOL_GROUP, POOL_GROUP), POOL_GROUP ** -0.5),
        "pool_scale": 0.5 + 0.1 * jax.random.normal(ks[14], (L, G), f),
        "w_out": nrm(ks[15], (L, MIX_WIDTH, D), MIX_WIDTH ** -0.5),
        "norm_ffn_g": gain(ks[16], (L, D)),
        "w_gate": nrm(ks[17], (L, D, D_FF), D ** -0.5),
        "w_up": nrm(ks[18], (L, D, D_FF), D ** -0.5),
        "w_down": nrm(ks[19], (L, D_FF, D), D_FF ** -0.5),
        "norm_ple_g": gain(ks[20], (L, D)),
        "w_ple_gate": nrm(ks[21], (L, D, D), D ** -0.5),
        "w_ple_proj": nrm(ks[22], (L, PLE_DIM, D), PLE_DIM ** -0.5),
        "final_norm_g": gain(ks[23], (D,)),
    }


def reference(x, p, norm_mix_g, w_in, sgu_ln_g, sgu_ln_b, sgu_w, sgu_b, sc_conv_w,
              cf_conv_w, cf_conv_b, cf_ln_g, cf_ln_b, pool_w, pool_scale, w_out,
              norm_ffn_g, w_gate, w_up, w_down, norm_ple_g, w_ple_gate, w_ple_proj,
              final_norm_g):
    h = x
    split_at = [A_COLS, A_COLS + B_COLS, A_COLS + B_COLS + C_COLS]
    for i in range(DEPTH):
        y = _rms(h, norm_mix_g[i])
        z = y @ w_in[i]
        za, zb, zc, zd = jnp.split(z, split_at, axis=-1)
        oa = _mixer_sgu(za, sgu_ln_g[i], sgu_ln_b[i], sgu_w[i], sgu_b[i])
        ob = _mixer_shortconv(zb, sc_conv_w[i])
        oc = _mixer_conformer(zc, cf_conv_w[i], cf_conv_b[i], cf_ln_g[i], cf_ln_b[i])
        od = _mixer_pool(zd, pool_w[i], pool_scale[i])
        h = h + jnp.concatenate([oa, ob, oc, od], axis=-1) @ w_out[i]
        y = _rms(h, norm_ffn_g[i])
        h = h + (jax.nn.silu(y @ w_gate[i]) * (y @ w_up[i])) @ w_down[i]
        y = _rms(h, norm_ple_g[i])
        h = h + jax.nn.sigmoid(y @ w_ple_gate[i]) * (p[i] @ w_ple_proj[i])
    return _rms(h, final_norm_g)
```

```python
import numpy as np
import concourse.bass as bass
import concourse.mybir as mybir
from concourse.bass_utils import run_bass_kernel_spmd

F32 = mybir.dt.float32
BF16 = mybir.dt.bfloat16
AF = mybir.ActivationFunctionType
ALU = mybir.AluOpType

D = 2048
KD = D // 128
FF = 5632
KF = FF // 128
PLE = 256
DEPTH = 4
SEQ = 4096
BATCH = 8
TT = 512
KDVE = 17
NSLOT = 8
SLOT_ELEMS = 16 * 128
EPS = 1e-6
GELU_C = 0.7978845608028654
POOL_W = (2, 4, 8, 16)

SL = 200
O_GMIX, O_GFFN, O_GPLE, O_SCW, O_CFW, O_CFB, O_LNG, O_LNB, O_PSC = 0, 16, 32, 48, 60, 184, 188, 192, 196


def smalls_size(L):
    return L * SL + 16 + 64 + 512 + 128


import os as _os0
SIGALL = set(_os0.environ.get('SIGALL', '').split(','))


class Inst:
    __slots__ = ("eng", "fn", "dma", "deps", "signal", "sigval", "idx")

    def __init__(self, eng, fn, dma):
        self.eng = eng
        self.fn = fn
        self.dma = dma
        self.deps = []
        self.signal = False
        self.sigval = 0


class Res:
    __slots__ = ("writer", "readers", "dma_readers")

    def __init__(self):
        self.writer = None
        self.readers = {}
        self.dma_readers = []


class Prog:
    ENGS = ("pe", "act", "dve", "pool", "sp")

    def __init__(self):
        self.streams = {e: [] for e in self.ENGS}
        self.res = {}
        self.dma_keys = []

    def _r(self, key):
        r = self.res.get(key)
        if r is None:
            r = self.res[key] = Res()
        return r

    def op(self, eng, fn, reads=(), writes=(), dma=None):
        inst = Inst(eng, fn, dma)
        if dma is not None and dma not in self.dma_keys:
            self.dma_keys.append(dma)
        deps = {}

        def add(d, raw):
            if d is None or d is inst:
                return
            if d.dma is None and inst.dma is None and d.eng == eng:
                if not raw or eng == "pe":
                    return
            if d.dma is not None:
                deps[id(d)] = d
            else:
                k = ("e", d.eng)
                o = deps.get(k)
                if o is None or o.idx < d.idx:
                    deps[k] = d

        for key in reads:
            add(self._r(key).writer, True)
        for key in writes:
            r = self._r(key)
            add(r.writer, False)
            for d in r.readers.values():
                add(d, False)
            for d in r.dma_readers:
                add(d, False)
        inst.idx = len(self.streams[eng])
        if eng in SIGALL:
            inst.signal = True
        self.streams[eng].append(inst)
        inst.deps = list(deps.values())
        for d in inst.deps:
            d.signal = True
        for key in reads:
            r = self._r(key)
            if dma is not None:
                r.dma_readers.append(inst)
            else:
                r.readers[eng] = inst
        for key in writes:
            r = self._r(key)
            r.writer = inst
            r.readers = {}
            r.dma_readers = []
        return inst

    def emit(self, nc, engines, sems, dma_sems):
        cnt = {e: 0 for e in self.ENGS}
        dcnt = {k: 0 for k in self.dma_keys}
        for e in self.ENGS:
            for inst in self.streams[e]:
                if inst.dma is not None:
                    dcnt[inst.dma] += 16
                    inst.sigval = dcnt[inst.dma]
                elif inst.signal:
                    cnt[e] += 1
                    inst.sigval = cnt[e]
        for e in self.ENGS:
            eng = engines[e]
            waited = {}
            for inst in self.streams[e]:
                need = {}
                for d in inst.deps:
                    k = ("d", d.dma) if d.dma is not None else ("e", d.eng)
                    if need.get(k, 0) < d.sigval:
                        need[k] = d.sigval
                for k, v in need.items():
                    if waited.get(k, 0) >= v:
                        continue
                    waited[k] = v
                    sem = dma_sems[k[1]] if k[0] == "d" else sems[k[1]]
                    eng.wait_ge(sem, v)
                bi = inst.fn(eng)
                if inst.dma is not None:
                    bi.then_inc(dma_sems[inst.dma], 16)
                elif inst.signal:
                    bi.then_inc(sems[e], 1)


def build_program(L, T, final_norm):
    NT = T // TT
    nc = bass.Bass("TRN2", target_bir_lowering=False)
    NS = smalls_size(L)
    O_FG = L * SL
    O_CNT = O_FG + 16
    O_MASK = O_CNT + 64
    O_ID = O_MASK + 512

    xT = nc.dram_tensor("xT", [D, T], F32, kind="ExternalInput").ap()
    pT = nc.dram_tensor("pT", [L, PLE, T], F32, kind="ExternalInput").ap()
    w_in_d = nc.dram_tensor("w_in", [L, 32, 128, 16, 128], F32, kind="ExternalInput").ap()
    w_out_d = nc.dram_tensor("w_out", [L, 16, 128, 16, 128], F32, kind="ExternalInput").ap()
    w_gu_d = nc.dram_tensor("w_gu", [L, 88, 128, 16, 128], F32, kind="ExternalInput").ap()
    w_dn_d = nc.dram_tensor("w_dn", [L, 16, 128, 44, 128], F32, kind="ExternalInput").ap()
    w_pg_d = nc.dram_tensor("w_pg", [L, 16, 128, 16, 128], F32, kind="ExternalInput").ap()
    w_pp_d = nc.dram_tensor("w_pp", [L, 4, 128, 2, 512], F32, kind="ExternalInput").ap()
    smalls_d = nc.dram_tensor("smalls", [128, NS], F32, kind="ExternalInput").ap()
    bc_d = nc.dram_tensor("bc", [L, 128, 3 * 512], F32, kind="ExternalInput").ap()
    sgw_d = nc.dram_tensor("sgw", [L, 128, 512], F32, kind="ExternalInput").ap()
    plw_d = nc.dram_tensor("plw", [L, 128, 512], F32, kind="ExternalInput").ap()
    outT = nc.dram_tensor("outT", [D, T], F32, kind="ExternalOutput").ap()
    import os
    DBG = bool(os.environ.get("KDBG"))
    if DBG:
        dbg = nc.dram_tensor("dbg", [128, 8, 542], F32, kind="ExternalOutput").ap()
        dbgw = nc.dram_tensor("dbgw", [128, 4, 2048], BF16, kind="ExternalOutput").ap()

    import os as _os
    P = Prog()
    from contextlib import ExitStack
    es = ExitStack()

    def sb(name, shape, dt):
        return es.enter_context(nc.sbuf_tensor("sb_" + name, shape, dt))

    h = sb("h", [128, KD, TT], F32)
    y = sb("y", [128, KD, TT], BF16)
    abig = sb("abig", [128, KF, TT], BF16)
    ring = sb("ring", [128, NSLOT, SLOT_ELEMS], BF16)
    pTb = sb("pTb", [128, 2, TT], BF16)
    smalls = sb("smalls", [128, NS], F32)
    bc = sb("bcs", [128, 3, 512], F32)
    sgw_f = sb("sgw_f", [128, 512], F32)
    sgw_b = sb("sgw_b", [128, 512], BF16)
    plw_b = sb("plw_b", [128, 512], BF16)
    ones_b = sb("ones_b", [128, 128], BF16)
    ones_c = sb("ones_c", [128, 128], BF16)
    epsT = sb("epsT", [128, 1], F32)
    lnscr = sb("lnscr", [128, 2], F32)
    stB = sb("stB", [128, L, 4, 2], F32)
    stC = sb("stC", [128, L, 4, 30], F32)
    stD = sb("stD", [128, L, 4, 16], F32)
    NSQ = 5
    sqb = sb("sqb", [128, NSQ, TT], BF16)
    rs = sb("rs", [128, TT], F32)
    NTMP = 6
    tmpf = sb("tmpf", [128, NTMP, TT], F32)
    g4 = [abig[:, 32 + 2 * tc_:34 + 2 * tc_, :].rearrange("p a b -> p (a b)").bitcast(F32) for tc_ in range(4)]
    st6 = sb("st6", [128, 16, 6], F32)
    mv = sb("mv", [128, 16, 2], F32)
    nr = sb("nr", [128, 4, 16], F32)
    vn = sb("vn", [128, 4, 512], BF16)
    u2 = sb("u2", [128, 2, TT], F32)
    qB = sb("qB", [128, 2, TT + 2], F32)
    accB = sb("accB", [128, 2, TT], F32)
    qC = sb("qC", [128, 4, TT + 32], BF16)
    convC = sb("convC", [128, 4, TT], F32)
    accC_l = [abig[:, 40 + 2 * i_:42 + 2 * i_, :].rearrange("p a b -> p (a b)").bitcast(F32) for i_ in range(2)]
    msb = sb("msb", [128, TT], F32)
    zD = sb("zD", [128, 2, TT + 16], F32)
    dA = sb("dA", [128, TT + 16], F32)
    dB = sb("dB", [128, TT + 16], F32)
    plD = sb("plD", [128, 4, TT], BF16)
    ps = [es.enter_context(nc.psum_tensor(f"ps{i}", [128, 512], F32)) for i in range(8)]

    ctr = {"bank": int(_os.environ.get("BANK0", "0")), "tmp": 0, "sq": 0}

    def next_bank():
        b = ctr["bank"]
        ctr["bank"] = (b + 1) % 7
        return b

    def next_tmp():
        i = ctr["tmp"]
        ctr["tmp"] = (i + 1) % NTMP
        return i

    def next_sq():
        i = ctr["sq"]
        ctr["sq"] = (i + 1) % NSQ
        return i

    units = []

    def build_units():
        for t in range(NT):
            for l in range(L):
                for u in range(32):
                    units.append((w_in_d[l, u], 16, 128))
                for u in range(16):
                    units.append((w_out_d[l, u], 16, 128))
                for u in range(88):
                    units.append((w_gu_d[l, u], 16, 128))
                for m in range(16):
                    units.append((w_dn_d[l, m, :, 0:16, :], 16, 128))
                    units.append((w_dn_d[l, m, :, 16:32, :], 16, 128))
                    units.append((w_dn_d[l, m, :, 32:44, :], 12, 128))
                for q in range(4):
                    for mm in range(4):
                        units.append((w_pg_d[l, q * 4 + mm], 16, 128))
                    units.append((w_pp_d[l, q], 2, 512))

    build_units()
    RING0 = int(_os.environ.get('RING0', '0'))
    ust = {"loaded": 0, "acq": 0, "rel": 0}

    def slot_view(slot, kc, ncols):
        return ring[:, slot, 0:kc * ncols].rearrange("p (k c) -> p k c", k=kc)

    def emit_loads():
        while ust["loaded"] < len(units) and ust["loaded"] < ust["rel"] + NSLOT:
            i = ust["loaded"]
            src, kc, ncols = units[i]
            slot = (i + RING0) % NSLOT
            dst = slot_view(slot, kc, ncols)
            P.op("pool", lambda g, dst=dst, src=src: g.dma_start(out=dst, in_=src),
                 writes=[("ring", slot)], dma=("ring", slot))
            ust["loaded"] += 1

    def acquire():
        i = ust["acq"]
        ust["acq"] += 1
        assert i < ust["loaded"], "unit not loaded (ring deadlock)"
        src, kc, ncols = units[i]
        slot = (i + RING0) % NSLOT
        return slot, slot_view(slot, kc, ncols), kc

    def release(n=1):
        ust["rel"] += n
        emit_loads()

    def S(col, n=1):
        return smalls[:, col:col + n]

    def gelu2(bank, out_ap, out_key):
        ix, isq = next_tmp(), next_tmp()
        xs, sq = tmpf[:, ix, :], tmpf[:, isq, :]
        P.op("act", lambda a: a.activation(out=xs, in_=ps[bank][:], func=AF.Copy),
             reads=[("ps", bank)], writes=[("tmp", ix)])
        P.op("act", lambda a: a.activation(out=sq, in_=ps[bank][:], func=AF.Square, scale=0.044715 ** 0.5),
             reads=[("ps", bank)], writes=[("tmp", isq)])
        P.op("dve", lambda v: v.scalar_tensor_tensor(out=sq, in0=sq, scalar=1.0, in1=xs, op0=ALU.add, op1=ALU.mult),
             reads=[("tmp", isq), ("tmp", ix)], writes=[("tmp", isq)])
        P.op("act", lambda a: a.activation(out=sq, in_=sq, func=AF.Tanh, scale=GELU_C),
             reads=[("tmp", isq)], writes=[("tmp", isq)])
        P.op("dve", lambda v: v.scalar_tensor_tensor(out=out_ap, in0=sq, scalar=1.0, in1=xs,
                                                     op0=ALU.add, op1=ALU.mult),
             reads=[("tmp", isq), ("tmp", ix)], writes=[out_key])

    def rsqrt_big(bank_or_ap, src_keys, scale=1.0):
        P.op("act", lambda a: a.activation(out=rs[:], in_=bank_or_ap, func=AF.Ln,
                                           bias=epsT[:, 0:1], scale=scale),
             reads=list(src_keys) + [("eps",)], writes=[("rs",)])
        P.op("act", lambda a: a.activation(out=rs[:], in_=rs[:], func=AF.Exp, scale=-0.5),
             reads=[("rs",)], writes=[("rs",)])

    SB = 7
    stat_state = {"pend": []}

    def stat_act(k):
        i = next_sq()
        P.op("act", lambda a, k=k, i=i: a.activation(out=sqb[:, i, :], in_=h[:, k, :], func=AF.Square),
             reads=[("h", k)], writes=[("sq", i)])
        stat_state["pend"].append((k, i))

    def stat_flush():
        for k, i in stat_state["pend"]:
            P.op("pe", lambda t_, k=k, i=i: t_.matmul(ps[SB][:], lhsT=ones_b[:], rhs=sqb[:, i, :],
                                                      start=(k == 0), stop=(k == KD - 1)),
                 reads=[("sq", i), ("ones_b",)], writes=[("ps", SB)])
        stat_state["pend"] = []

    def ln_preload():
        P.op("act", lambda a: a.activation(out=lnscr[:, 0:1], in_=epsT[:, 0:1], func=AF.Ln),
             reads=[("eps",)], writes=[("lnscr",)])

    def norm_finish(gcol, dst_h=False):
        stat_flush()
        rsqrt_big(ps[SB][:], [("ps", SB)])
        for k in range(KD):
            if dst_h:
                P.op("dve", lambda v, k=k: v.scalar_tensor_tensor(out=h[:, k, :], in0=h[:, k, :],
                                                                  scalar=S(gcol + k), in1=rs[:],
                                                                  op0=ALU.mult, op1=ALU.mult),
                     reads=[("h", k), ("rs",), ("smalls",)], writes=[("h", k)])
            else:
                P.op("dve", lambda v, k=k: v.scalar_tensor_tensor(out=y[:, k, :], in0=h[:, k, :],
                                                                  scalar=S(gcol + k), in1=rs[:],
                                                                  op0=ALU.mult, op1=ALU.mult),
                     reads=[("h", k), ("rs",), ("smalls",)], writes=[("y", k)])

    def mm_unit(bank, uview, kc, rhs_of_k, rhs_keys, slot, first=True, last=True, k0=0, korder=None):
        ks = list(korder) if korder is not None else list(range(kc))
        for n_, k in enumerate(ks):
            P.op("pe", lambda t, k=k, n_=n_: t.matmul(ps[bank][:], lhsT=uview[:, k, :], rhs=rhs_of_k(k0 + k),
                                                      start=(first and n_ == 0), stop=(last and n_ == kc - 1)),
                 reads=[("ring", slot)] + [rk(k0 + k) for rk in rhs_keys], writes=[("ps", bank)])

    def y_group(dbgi=None):
        slot, uv, kc = acquire()
        bank = next_bank()
        mm_unit(bank, uv, kc, lambda k: y[:, k, :], [lambda k: ("y", k)], slot)
        if dbgi is not None:
            P.op("sp", lambda s_: s_.dma_start(out=dbgw[:, dbgi, :], in_=ring[:, slot, :]),
                 reads=[("ring", slot)], writes=[("dbgw", dbgi)], dma=("dbgw",))
        release()
        return bank

    def y_groups_kmajor(n):
        us = [acquire() for _ in range(n)]
        banks = [next_bank() for _ in range(n)]
        for k in range(KD):
            for (slot, uv, kc), bank in zip(us, banks):
                P.op("pe", lambda t_, k=k, uv=uv, bank=bank: t_.matmul(ps[bank][:], lhsT=uv[:, k, :], rhs=y[:, k, :],
                                                                       start=(k == 0), stop=(k == KD - 1)),
                     reads=[("ring", slot), ("y", k)], writes=[("ps", bank)])
        release(n)
        return banks

    P.op("sp", lambda s: s.dma_start(out=smalls[:], in_=smalls_d), writes=[("smalls",)], dma=("smalls",))
    P.op("dve", lambda v: v.memset(ones_b[:], 1.0 / D), writes=[("ones_b",)])
    P.op("dve", lambda v: v.memset(ones_c[:], 1.0 / 512), writes=[("ones_c",)])
    P.op("dve", lambda v: v.memset(epsT[:], EPS), writes=[("eps",)])
    P.op("dve", lambda v: v.memset(stB[:], 0.0), writes=[("stB", l_, c_) for l_ in range(L) for c_ in range(4)])
    P.op("dve", lambda v: v.memset(stC[:], 0.0), writes=[("stC", l_, c_) for l_ in range(L) for c_ in range(4)])
    P.op("dve", lambda v: v.memset(stD[:], 0.0), writes=[("stD", l_, c_) for l_ in range(L) for c_ in range(4)])
    for l in range(L):
        c0 = l * SL + O_CFW
        P.op("dve", lambda v, c0=c0: v.tensor_scalar(out=smalls[:, c0:c0 + 124], in0=smalls[:, c0:c0 + 124],
                                                     scalar1=0.5, scalar2=None, op0=ALU.mult),
             reads=[("smalls",)], writes=[("smalls",)])
    emit_loads()

    xTv = xT.rearrange("(k p) t -> p k t", p=128)
    oTv = outT.rearrange("(k p) t -> p k t", p=128)


    def diag_ap(c_, k):
        i_ = c_ * (31 - KDVE) + (k - KDVE)
        j = 16 + i_ // 4
        return abig[:, j, (i_ % 4) * 128:(i_ % 4 + 1) * 128], ("a", j)

    def diag_build_thunks(l_):
        th_ = []
        for c_ in range(4):
            wc = l_ * SL + O_CFW + c_ * 31
            for k in range(KDVE, 31):
                dap, dkey = diag_ap(c_, k)
                th_.append(lambda dap=dap, dkey=dkey, col=wc + k: P.op(
                    "act", lambda a: a.activation(out=dap, in_=smalls[:, O_ID:O_ID + 128], func=AF.Identity,
                                                  scale=S(col)),
                    reads=[("smalls",)], writes=[dkey]))
        return th_

    for th_ in diag_build_thunks(0):
        th_()
    pending = []

    for t in range(NT):
        tok0 = t * TT
        for kq in range(4):
            P.op("sp", lambda s, kq=kq, tok0=tok0: s.dma_start(out=h[:, kq * 4:(kq + 1) * 4, :],
                                                    in_=xTv[:, kq * 4:(kq + 1) * 4, tok0:tok0 + TT]),
                 writes=[("h", k) for k in range(kq * 4, kq * 4 + 4)], dma=("x", kq))
        for k in range(KD):
            stat_act(k)
            stat_flush()
        for l in range(L):
            so = l * SL
            P.op("sp", lambda s, l=l: s.dma_start(out=bc[:].rearrange("p a b -> p (a b)"), in_=bc_d[l]),
                 writes=[("bc",)], dma=("bc",))
            P.op("sp", lambda s, l=l: s.dma_start(out=sgw_f[:], in_=sgw_d[l]), writes=[("sgw_f",)], dma=("sgw",))
            P.op("dve", lambda v: v.tensor_tensor(out=sgw_b[:], in0=sgw_f[:], in1=smalls[:, O_MASK:O_MASK + 512],
                                                  op=ALU.mult),
                 reads=[("sgw_f",), ("smalls",)], writes=[("sgw_b",)])
            P.op("pool", lambda g, l=l: g.dma_start(out=plw_b[:], in_=plw_d[l]), writes=[("plw_b",)], dma=("plw",))
            P.op("pool", lambda g, l=l, tok0=tok0: g.dma_start(
                out=pTb[:], in_=pT[l].rearrange("(k p) t -> p k t", p=128)[:, :, tok0:tok0 + TT]),
                writes=[("pTb",)], dma=("pTb",))

            norm_finish(so + O_GMIX)

            while pending:
                pending.pop(0)()
            vs = [acquire() for _ in range(4)]
            s0 = vs[0][0]
            assert [s_[0] for s_ in vs] == [s0, s0 + 1, s0 + 2, s0 + 3] and s0 + 3 < NSLOT
            vbanks = [next_bank() for _ in range(4)]
            for k in range(KD):
                for tc in range(4):
                    P.op("pe", lambda tt, k=k, tc=tc, bank=vbanks[tc], s0=s0: tt.matmul(
                        ps[bank][:].rearrange("p (a b) -> p a b", a=4),
                        lhsT=y[:, k, tc * 128:(tc + 1) * 128],
                        rhs=ring[:, s0:s0 + 4, k * 128:(k + 1) * 128],
                        start=(k == 0), stop=(k == KD - 1)),
                        reads=[("ring", s0), ("ring", s0 + 1), ("ring", s0 + 2), ("ring", s0 + 3), ("y", k)],
                        writes=[("ps", vbanks[tc])])
            for tc in range(4):
                bank = vbanks[tc]
                gelu2(bank, g4[tc], ("g4", tc))
                for hd in range(4):
                    j = tc * 4 + hd
                    P.op("dve", lambda v, tc=tc, hd=hd, j=j: v.bn_stats(out=st6[:, j, :],
                                                                        in_=g4[tc][:, hd * 128:(hd + 1) * 128]),
                         reads=[("g4", tc)], writes=[("st6", j)])
                for hd in range(4):
                    j = tc * 4 + hd
                    P.op("dve", lambda v, j=j: v.bn_aggr(out=mv[:, j, :], in_=st6[:, j, :]),
                         reads=[("st6", j)], writes=[("mv",)])
            release(4)
            for c in range(4):
                zi = c % 2
                w = POOL_W[c]
                bz = y_group()
                P.op("act", lambda a, zi=zi, l=l, c=c: a.activation(out=zD[:, zi, 0:16], in_=stD[:, l, c, :],
                                                                     func=AF.Copy),
                     reads=[("stD", l, c)], writes=[("zDh", zi)])
                P.op("act", lambda a, zi=zi, bz=bz: a.activation(out=zD[:, zi, 16:16 + TT], in_=ps[bz][:],
                                                                 func=AF.Copy),
                     reads=[("ps", bz)], writes=[("zD", zi)])
                P.op("act", lambda a, zi=zi, l=l, c=c: a.activation(out=stD[:, l, c, :], in_=zD[:, zi, TT:TT + 16],
                                                                     func=AF.Copy),
                     reads=[("zD", zi)], writes=[("stD", l, c)])
                cur, cur_key, lo = zD[:, zi, :], [("zD", zi), ("zDh", zi)], 0
                bufs = [(dA, ("dA",)), (dB, ("dB",))]
                for j in range(c + 1):
                    sh = 1 << j
                    nb_, nk = bufs[j % 2]
                    nlo = lo + sh
                    P.op("dve", lambda v, cur=cur, nb_=nb_, nlo=nlo, sh=sh: v.tensor_tensor(
                        out=nb_[:, nlo:TT + 16], in0=cur[:, nlo:TT + 16], in1=cur[:, nlo - sh:TT + 16 - sh],
                        op=ALU.add),
                        reads=list(cur_key), writes=[nk])
                    cur, cur_key, lo = nb_[:], [nk], nlo
                P.op("dve", lambda v, cur=cur, zi=zi, w=w, c=c: v.scalar_tensor_tensor(
                    out=plD[:, c, :], in0=cur[:, 16:16 + TT], scalar=1.0 / w, in1=zD[:, zi, 16:16 + TT],
                    op0=ALU.mult, op1=ALU.subtract),
                    reads=list(cur_key) + [("zD", zi)], writes=[("plD", c)])
                if t == 0:
                    i16 = next_tmp()
                    t16 = tmpf[:, i16, 0:16]
                    P.op("dve", lambda v, cur=cur, t16=t16, c=c: v.tensor_tensor(
                        out=t16, in0=cur[:, 16:32], in1=smalls[:, O_CNT + c * 16:O_CNT + c * 16 + 16], op=ALU.mult),
                        reads=list(cur_key) + [("smalls",)], writes=[("tmp", i16)])
                    P.op("dve", lambda v, t16=t16, zi=zi, c=c: v.tensor_tensor(
                        out=plD[:, c, 0:16], in0=t16, in1=zD[:, zi, 16:32], op=ALU.subtract),
                        reads=[("tmp", i16), ("zD", zi)], writes=[("plD", c)])

            def conv_c(c):
                bcv = next_bank()
                for k in range(KDVE, 31):
                    dap, dkey = diag_ap(c, k)
                    P.op("pe", lambda tt, c=c, k=k, dap=dap, bcv=bcv: tt.matmul(
                        ps[bcv][:], lhsT=dap, rhs=qC[:, c, k:k + TT], start=(k == KDVE), stop=(k == 30)),
                        reads=[dkey, ("qC", c), ("qCh", c)], writes=[("ps", bcv)])
                P.op("dve", lambda v, c=c, bcv=bcv, so=so: v.scalar_tensor_tensor(
                    out=convC[:, c, :], in0=ps[bcv][:], scalar=S(so + O_CFB + c), in1=convC[:, c, :],
                    op0=ALU.add, op1=ALU.add),
                    reads=[("ps", bcv), ("convC", c), ("smalls",)], writes=[("convC", c)])
                P.op("dve", lambda v, c=c: v.tensor_tensor(out=convC[:, c, :], in0=convC[:, c, :], in1=accC_l[c % 2],
                                                           op=ALU.add),
                     reads=[("convC", c), ("accC", c % 2)], writes=[("convC", c)])

            def conv_dve(c):
                wc = so + O_CFW + c * 31
                ai = c % 2
                P.op("dve", lambda v, c=c, wc=wc: v.tensor_scalar(
                    out=convC[:, c, :], in0=qC[:, c, 0:TT], scalar1=S(wc), scalar2=None, op0=ALU.mult),
                    reads=[("qC", c), ("qCh", c), ("smalls",)], writes=[("convC", c)])
                P.op("dve", lambda v, c=c, wc=wc, ai=ai: v.tensor_scalar(
                    out=accC_l[ai], in0=qC[:, c, 1:1 + TT], scalar1=S(wc + 1), scalar2=None, op0=ALU.mult),
                    reads=[("qC", c), ("qCh", c), ("smalls",)], writes=[("accC", ai)])
                for kk in range(2, KDVE):
                    if kk % 2 == 0:
                        P.op("dve", lambda v, c=c, wc=wc, kk=kk: v.scalar_tensor_tensor(
                            out=convC[:, c, :], in0=qC[:, c, kk:kk + TT], scalar=S(wc + kk), in1=convC[:, c, :],
                            op0=ALU.mult, op1=ALU.add),
                            reads=[("qC", c), ("qCh", c), ("convC", c), ("smalls",)], writes=[("convC", c)])
                    else:
                        P.op("dve", lambda v, c=c, wc=wc, kk=kk, ai=ai: v.scalar_tensor_tensor(
                            out=accC_l[ai], in0=qC[:, c, kk:kk + TT], scalar=S(wc + kk), in1=accC_l[ai],
                            op0=ALU.mult, op1=ALU.add),
                            reads=[("qC", c), ("qCh", c), ("accC", ai), ("smalls",)], writes=[("accC", ai)])

            for c in range(4):
                bg_ = y_group()
                ig = next_tmp()
                th = tmpf[:, ig, :]
                P.op("act", lambda a, bg_=bg_, th=th: a.activation(out=th, in_=ps[bg_][:], func=AF.Tanh, scale=0.5),
                     reads=[("ps", bg_)], writes=[("tmp", ig)])
                ba = y_group()
                P.op("act", lambda a, l=l, c=c: a.activation(out=qC[:, c, 0:30], in_=stC[:, l, c, :], func=AF.Copy),
                     reads=[("stC", l, c)], writes=[("qCh", c)])
                P.op("dve", lambda v, c=c, ba=ba, th=th: v.scalar_tensor_tensor(
                    out=qC[:, c, 30:30 + TT], in0=th, scalar=1.0, in1=ps[ba][:], op0=ALU.add, op1=ALU.mult),
                    reads=[("ps", ba), ("tmp", ig)], writes=[("qC", c)])
                P.op("act", lambda a, l=l, c=c: a.activation(out=stC[:, l, c, :], in_=qC[:, c, TT:TT + 30],
                                                             func=AF.Copy),
                     reads=[("qC", c)], writes=[("stC", l, c)])
                conv_dve(c)
                if c >= 1:
                    conv_c(c - 1)

            nx, ny, na, nb = nr[:, 0, :], nr[:, 1, :], nr[:, 2, :], nr[:, 3, :]
            P.op("dve", lambda v: v.tensor_scalar(out=nx, in0=mv[:, :, 1], scalar1=4.0 * EPS, scalar2=None,
                                                  op0=ALU.add), reads=[("mv",)], writes=[("nx",)])
            P.op("dve", lambda v: v.tensor_scalar(out=na, in0=nx, scalar1=0.5, scalar2=0.5, op0=ALU.mult,
                                                  op1=ALU.add), reads=[("nx",)], writes=[("na",)])
            P.op("dve", lambda v: v.reciprocal(out=ny, in_=na), reads=[("na",)], writes=[("ny",)])
            for it in range(6):
                P.op("dve", lambda v: v.tensor_tensor(out=na, in0=ny, in1=ny, op=ALU.mult),
                     reads=[("ny",)], writes=[("na",)])
                P.op("dve", lambda v: v.scalar_tensor_tensor(out=nb, in0=na, scalar=-0.5, in1=nx,
                                                             op0=ALU.mult, op1=ALU.mult),
                     reads=[("na",), ("nx",)], writes=[("nb",)])
                P.op("dve", lambda v: v.scalar_tensor_tensor(out=ny, in0=nb, scalar=1.5, in1=ny,
                                                             op0=ALU.add, op1=ALU.mult),
                     reads=[("nb",), ("ny",)], writes=[("ny",)])
            for tc in range(4):
                it_ = next_tmp()
                tv = tmpf[:, it_, :]
                for hd in range(4):
                    j = tc * 4 + hd
                    P.op("dve", lambda v, tc=tc, hd=hd, j=j, tv=tv: v.tensor_scalar(
                        out=tv[:, hd * 128:(hd + 1) * 128], in0=g4[tc][:, hd * 128:(hd + 1) * 128],
                        scalar1=mv[:, j, 0:1], scalar2=nr[:, 1, j:j + 1], op0=ALU.subtract, op1=ALU.mult),
                        reads=[("g4", tc), ("mv",), ("ny",)], writes=[("tmp", it_)])
                P.op("dve", lambda v, tv=tv: v.tensor_tensor(out=tv, in0=tv, in1=bc[:, 0, :], op=ALU.mult),
                     reads=[("tmp", it_), ("bc",)], writes=[("tmp", it_)])
                P.op("dve", lambda v, tv=tv, tc=tc: v.tensor_tensor(out=vn[:, tc, :], in0=tv, in1=bc[:, 1, :],
                                                                    op=ALU.add),
                     reads=[("tmp", it_), ("bc",)], writes=[("vn", tc)])

            for c in range(4):
                qi = c % 2
                bh = y_group()
                if c == 0:
                    conv_c(3)
                ih = next_tmp()
                hs = tmpf[:, ih, :]
                P.op("act", lambda a, bh=bh, hs=hs: a.activation(out=hs, in_=ps[bh][:], func=AF.Copy),
                     reads=[("ps", bh)], writes=[("tmp", ih)])
                bcg = y_group()
                P.op("act", lambda a, qi=qi, l=l, c=c: a.activation(out=qB[:, qi, 0:2], in_=stB[:, l, c, :],
                                                                     func=AF.Copy),
                     reads=[("stB", l, c)], writes=[("qBh", qi)])
                P.op("dve", lambda v, qi=qi, bcg=bcg, hs=hs: v.tensor_tensor(out=qB[:, qi, 2:2 + TT], in0=ps[bcg][:],
                                                                             in1=hs, op=ALU.mult),
                     reads=[("ps", bcg), ("tmp", ih)], writes=[("qB", qi)])
                P.op("act", lambda a, qi=qi, l=l, c=c: a.activation(out=stB[:, l, c, :], in_=qB[:, qi, TT:TT + 2],
                                                                     func=AF.Copy),
                     reads=[("qB", qi)], writes=[("stB", l, c)])
                wc = so + O_SCW + c * 3
                P.op("dve", lambda v, qi=qi, wc=wc: v.tensor_scalar(out=accB[:, qi, :], in0=qB[:, qi, 0:TT],
                                                                    scalar1=S(wc), scalar2=None, op0=ALU.mult),
                     reads=[("qB", qi), ("qBh", qi), ("smalls",)], writes=[("accB", qi)])
                for kk in (1, 2):
                    P.op("dve", lambda v, qi=qi, wc=wc, kk=kk: v.scalar_tensor_tensor(
                        out=accB[:, qi, :], in0=qB[:, qi, kk:kk + TT], scalar=S(wc + kk), in1=accB[:, qi, :],
                        op0=ALU.mult, op1=ALU.add),
                        reads=[("qB", qi), ("qBh", qi), ("accB", qi), ("smalls",)], writes=[("accB", qi)])
                bbg = y_group()
                P.op("dve", lambda v, qi=qi, bbg=bbg, c=c: v.tensor_tensor(out=abig[:, 4 + c, :], in0=ps[bbg][:],
                                                                           in1=accB[:, qi, :], op=ALU.mult),
                     reads=[("ps", bbg), ("accB", qi)], writes=[("a", 4 + c)])

            for hd in range(4):
                bank = y_group()
                ui = hd % 2
                gelu2(bank, u2[:, ui, :], ("u2", ui))
                bank2 = next_bank()
                for tc in range(4):
                    P.op("pe", lambda tt, tc=tc, hd=hd, bank2=bank2: tt.matmul(
                        ps[bank2][:, tc * 128:(tc + 1) * 128], lhsT=vn[:, tc, hd * 128:(hd + 1) * 128],
                        rhs=sgw_b[:, hd * 128:(hd + 1) * 128], start=True, stop=True),
                        reads=[("vn", tc), ("sgw_b",)], writes=[("ps", bank2)])
                it_ = next_tmp()
                tv = tmpf[:, it_, :]
                P.op("dve", lambda v, tv=tv, hd=hd, bank2=bank2: v.tensor_tensor(
                    out=tv.rearrange("p (a b) -> p a b", a=4),
                    in0=ps[bank2][:].rearrange("p (a b) -> p a b", a=4),
                    in1=bc[:, 2, hd * 128:(hd + 1) * 128].unsqueeze(1).broadcast_to([128, 4, 128]),
                    op=ALU.add),
                    reads=[("ps", bank2), ("bc",)], writes=[("tmp", it_)])
                P.op("dve", lambda v, tv=tv, hd=hd, ui=ui: v.scalar_tensor_tensor(
                    out=abig[:, hd, :], in0=tv, scalar=0.5, in1=u2[:, ui, :], op0=ALU.mult, op1=ALU.mult),
                    reads=[("tmp", it_), ("u2", ui)], writes=[("a", hd)])

            for c in range(4):
                bp = next_bank()
                P.op("pe", lambda tt, c=c, bp=bp: tt.matmul(ps[bp][:], lhsT=plw_b[:, c * 128:(c + 1) * 128],
                                                            rhs=plD[:, c, :], start=True, stop=True),
                     reads=[("plw_b",), ("plD", c)], writes=[("ps", bp)])
                P.op("act", lambda a, c=c, bp=bp, so=so: a.activation(out=abig[:, 12 + c, :], in_=ps[bp][:],
                                                                      func=AF.Identity, scale=S(so + O_PSC + c)),
                     reads=[("ps", bp), ("smalls",)], writes=[("a", 12 + c)])

            b1, b2 = next_bank(), next_bank()
            for c in range(4):
                isq = next_tmp()
                icb = next_tmp()
                sq = tmpf[:, isq, :].bitcast(BF16)[:, 0:TT]
                cb = tmpf[:, icb, :].bitcast(BF16)[:, 0:TT]
                P.op("act", lambda a, c=c, cb=cb: a.activation(out=cb, in_=convC[:, c, :], func=AF.Copy),
                     reads=[("convC", c)], writes=[("tmp", icb)])
                P.op("act", lambda a, c=c, sq=sq: a.activation(out=sq, in_=convC[:, c, :], func=AF.Square),
                     reads=[("convC", c)], writes=[("tmp", isq)])
                P.op("pe", lambda tt, c=c, b1=b1, cb=cb: tt.matmul(ps[b1][:], lhsT=ones_c[:], rhs=cb,
                                                                   start=(c == 0), stop=(c == 3)),
                     reads=[("tmp", icb), ("ones_c",)], writes=[("ps", b1)])
                P.op("pe", lambda tt, c=c, sq=sq, b2=b2: tt.matmul(ps[b2][:], lhsT=ones_c[:], rhs=sq,
                                                                   start=(c == 0), stop=(c == 3)),
                     reads=[("tmp", isq), ("ones_c",)], writes=[("ps", b2)])
            iv = next_tmp()
            var = tmpf[:, iv, :]
            P.op("act", lambda a, b1=b1: a.activation(out=msb[:], in_=ps[b1][:], func=AF.Copy),
                 reads=[("ps", b1)], writes=[("msb",)])
            P.op("dve", lambda v, var=var: v.tensor_tensor(out=var, in0=msb[:], in1=msb[:], op=ALU.mult),
                 reads=[("msb",)], writes=[("tmp", iv)])
            P.op("dve", lambda v, var=var, b2=b2: v.tensor_tensor(out=var, in0=ps[b2][:], in1=var, op=ALU.subtract),
                 reads=[("ps", b2), ("tmp", iv)], writes=[("tmp", iv)])
            rsqrt_big(var, [("tmp", iv)])
            for c in range(4):
                ix = next_tmp()
                xc = tmpf[:, ix, :]
                P.op("dve", lambda v, c=c, xc=xc: v.tensor_tensor(out=xc, in0=convC[:, c, :], in1=msb[:],
                                                                  op=ALU.subtract),
                     reads=[("convC", c), ("msb",)], writes=[("tmp", ix)])
                P.op("dve", lambda v, xc=xc: v.tensor_tensor(out=xc, in0=xc, in1=rs[:], op=ALU.mult),
                     reads=[("tmp", ix), ("rs",)], writes=[("tmp", ix)])
                P.op("act", lambda a, c=c, xc=xc, so=so: a.activation(out=abig[:, 8 + c, :], in_=xc, func=AF.Silu,
                                                               scale=S(so + O_LNG + c), bias=S(so + O_LNB + c)),
                     reads=[("tmp", ix), ("smalls",)], writes=[("a", 8 + c)])

            ln_preload()
            for m in range(KD):
                slot, uv, kc = acquire()
                bank = next_bank()
                mm_unit(bank, uv, kc, lambda k: abig[:, k, :], [lambda k: ("a", k)], slot,
                        korder=list(range(4, 16)) + list(range(4)))
                release()
                stat_flush()
                P.op("dve", lambda v, m=m, bank=bank: v.tensor_tensor(out=h[:, m, :], in0=ps[bank][:],
                                                                      in1=h[:, m, :], op=ALU.add),
                     reads=[("ps", bank), ("h", m)], writes=[("h", m)])
                stat_act(m)

            norm_finish(so + O_GFFN)
            ffn_banks = y_groups_kmajor(4)
            for j in range(KF):
                bgt = ffn_banks[2 * j] if j < 2 else y_group()
                isg = next_tmp()
                sg = tmpf[:, isg, :]
                P.op("act", lambda a, bgt=bgt, sg=sg: a.activation(out=sg, in_=ps[bgt][:], func=AF.Silu),
                     reads=[("ps", bgt)], writes=[("tmp", isg)])
                bup = ffn_banks[2 * j + 1] if j < 2 else y_group()
                P.op("dve", lambda v, j=j, bup=bup, sg=sg: v.tensor_tensor(out=abig[:, j, :], in0=ps[bup][:],
                                                                           in1=sg, op=ALU.mult),
                     reads=[("ps", bup), ("tmp", isg)], writes=[("a", j)])
            ln_preload()
            for m in range(KD):
                bank = next_bank()
                k0 = 0
                for part in range(3):
                    slot, uv, kc = acquire()
                    mm_unit(bank, uv, kc, lambda k: abig[:, k, :], [lambda k: ("a", k)], slot,
                            first=(part == 0), last=(part == 2), k0=k0)
                    k0 += kc
                    release()
                stat_flush()
                P.op("dve", lambda v, m=m, bank=bank: v.tensor_tensor(out=h[:, m, :], in0=ps[bank][:],
                                                                      in1=h[:, m, :], op=ALU.add),
                     reads=[("ps", bank), ("h", m)], writes=[("h", m)])
                stat_act(m)

            norm_finish(so + O_GPLE)
            pending = diag_build_thunks((l + 1) % L) if not (t == NT - 1 and l == L - 1) else []
            need_stats = (l < L - 1) or final_norm
            ple_banks = y_groups_kmajor(4)
            for q in range(4):
                for mm in range(4):
                    bgt = ple_banks[mm] if q == 0 else y_group()
                    P.op("act", lambda a, bgt=bgt, mm=mm: a.activation(out=g4[mm], in_=ps[bgt][:],
                                                                       func=AF.Tanh, scale=0.5),
                         reads=[("ps", bgt)], writes=[("g4", mm)])
                    for _ in range(4):
                        if pending:
                            pending.pop(0)()
                stat_flush()
                slot, uv, kc = acquire()
                for mm in range(4):
                    m = q * 4 + mm
                    bank = next_bank()
                    for kk in range(2):
                        P.op("pe", lambda tt, kk=kk, mm=mm, bank=bank, uv=uv: tt.matmul(
                            ps[bank][:], lhsT=uv[:, kk, mm * 128:(mm + 1) * 128], rhs=pTb[:, kk, :],
                            start=(kk == 0), stop=(kk == 1)),
                            reads=[("ring", slot), ("pTb",)], writes=[("ps", bank)])
                    itp = next_tmp()
                    tp = tmpf[:, itp, :]
                    P.op("dve", lambda v, mm=mm, bank=bank, tp=tp: v.scalar_tensor_tensor(
                        out=tp, in0=g4[mm], scalar=1.0, in1=ps[bank][:], op0=ALU.add, op1=ALU.mult),
                        reads=[("g4", mm), ("ps", bank)], writes=[("tmp", itp)])
                    P.op("dve", lambda v, m=m, tp=tp: v.scalar_tensor_tensor(
                        out=h[:, m, :], in0=tp, scalar=0.5, in1=h[:, m, :], op0=ALU.mult, op1=ALU.add),
                        reads=[("tmp", itp), ("h", m)], writes=[("h", m)])
                    if need_stats:
                        stat_act(m)
                release()

        if final_norm:
            norm_finish(O_FG, dst_h=True)
        for kq in range(4):
            P.op("sp", lambda s, kq=kq, tok0=tok0: s.dma_start(out=oTv[:, kq * 4:(kq + 1) * 4, tok0:tok0 + TT],
                                                    in_=h[:, kq * 4:(kq + 1) * 4, :]),
                 reads=[("h", k) for k in range(kq * 4, kq * 4 + 4)], writes=[("out", t, kq)], dma=("o", kq))
    P.op("sp", lambda s: s.nop(), reads=[("out", t, kq) for t in range(NT) for kq in range(4)])
    assert ust["acq"] == len(units) and ust["loaded"] == len(units)

    sem_stack = ExitStack()
    sems = {e: sem_stack.enter_context(nc.semaphore(f"s_{e}")) for e in Prog.ENGS}
    dma_sems = {k: sem_stack.enter_context(nc.semaphore("d_" + "_".join(str(x) for x in k))) for k in P.dma_keys}
    with nc.Block() as block:
        engines = {}

        @block.tensor
        def _(e):
            P.emit_one(nc, "pe", e, sems, dma_sems)

        @block.scalar
        def _(e):
            P.emit_one(nc, "act", e, sems, dma_sems)

        @block.vector
        def _(e):
            P.emit_one(nc, "dve", e, sems, dma_sems)

        @block.gpsimd
        def _(e):
            P.emit_one(nc, "pool", e, sems, dma_sems)

        @block.sync
        def _(e):
            P.emit_one(nc, "sp", e, sems, dma_sems)
    sem_stack.close()
    es.close()
    return nc


def _assign_sigvals(self):
    cnt = {e: 0 for e in self.ENGS}
    dcnt = {k: 0 for k in self.dma_keys}
    for e in self.ENGS:
        for inst in self.streams[e]:
            if inst.dma is not None:
                dcnt[inst.dma] += 16
                inst.sigval = dcnt[inst.dma]
            elif inst.signal:
                cnt[e] += 1
                inst.sigval = cnt[e]
    self._assigned = True


def _emit_one(self, nc, e, eng, sems, dma_sems):
    if not getattr(self, "_assigned", False):
        _assign_sigvals(self)
    waited = {}
    for inst in self.streams[e]:
        need = {}
        for d in inst.deps:
            k = ("d", d.dma) if d.dma is not None else ("e", d.eng)
            if need.get(k, 0) < d.sigval:
                need[k] = d.sigval
        for k, v in need.items():
            if waited.get(k, 0) >= v:
                continue
            waited[k] = v
            sem = dma_sems[k[1]] if k[0] == "d" else sems[k[1]]
            eng.wait_ge(sem, v)
        bi = inst.fn(eng)
        if inst.dma is not None:
            bi.then_inc(dma_sems[inst.dma], 16)
        elif inst.signal:
            bi.then_inc(sems[e], 1)


Prog.emit_one = _emit_one


def _units(w, cols):
    K = w.shape[0] // 128
    wv = w.reshape(K, 128, w.shape[1] // 128, 128)
    idx = np.asarray([c // 128 for c in cols])
    return np.ascontiguousarray(wv[:, :, idx, :].transpose(2, 1, 0, 3))


W_IN_COLS = ([512 + 128 * i for i in range(4)]
             + [3584 + 128 * c for c in range(4)]
             + [x for c in range(4) for x in (2560 + 512 + 128 * c, 2560 + 128 * c)]
             + [x for c in range(4) for x in (1024 + 128 * c, 1024 + 1024 + 128 * c, 1024 + 512 + 128 * c)]
             + [128 * i for i in range(4)])


def prep_weights(inp, layers):
    L = len(layers)
    f = np.float32
    out = {}
    out["w_in"] = np.stack([_units(inp["w_in"][l], W_IN_COLS) for l in layers])
    out["w_out"] = np.stack([_units(inp["w_out"][l], [128 * i for i in range(16)]) for l in layers])
    gu = []
    for l in layers:
        g = _units(inp["w_gate"][l], [128 * i for i in range(44)])
        u = _units(inp["w_up"][l], [128 * i for i in range(44)])
        gu.append(np.stack([g, u], axis=1).reshape(88, 128, 16, 128))
    out["w_gu"] = np.stack(gu)
    out["w_dn"] = np.stack([_units(inp["w_down"][l], [128 * i for i in range(16)]) for l in layers])
    out["w_pg"] = np.stack([_units(inp["w_ple_gate"][l], [128 * i for i in range(16)]) for l in layers])
    pp = []
    for l in layers:
        w = inp["w_ple_proj"][l].reshape(2, 128, 4, 512)
        pp.append(np.ascontiguousarray(w.transpose(2, 1, 0, 3)))
    out["w_pp"] = np.stack(pp)
    NS = smalls_size(L)
    sm = np.zeros((128, NS), f)

    def pk(v):
        return v.reshape(-1, 128).T

    for li, l in enumerate(layers):
        o = li * SL
        sm[:, o + O_GMIX:o + O_GMIX + 16] = pk(inp["norm_mix_g"][l])
        sm[:, o + O_GFFN:o + O_GFFN + 16] = pk(inp["norm_ffn_g"][l])
        sm[:, o + O_GPLE:o + O_GPLE + 16] = pk(inp["norm_ple_g"][l])
        sm[:, o + O_SCW:o + O_SCW + 12] = inp["sc_conv_w"][l].reshape(3, 4, 128).transpose(2, 1, 0).reshape(128, 12)
        sm[:, o + O_CFW:o + O_CFW + 124] = inp["cf_conv_w"][l].reshape(31, 4, 128).transpose(2, 1, 0).reshape(128, 124)
        sm[:, o + O_CFB:o + O_CFB + 4] = pk(inp["cf_conv_b"][l])
        sm[:, o + O_LNG:o + O_LNG + 4] = pk(inp["cf_ln_g"][l])
        sm[:, o + O_LNB:o + O_LNB + 4] = pk(inp["cf_ln_b"][l])
        sm[:, o + O_PSC:o + O_PSC + 4] = pk(inp["pool_scale"][l])
    o = L * SL
    sm[:, o:o + 16] = pk(inp["final_norm_g"])
    pos = np.arange(1, 17, dtype=f)
    cnt = np.stack([1.0 / np.minimum(pos, w) for w in POOL_W]).astype(f)
    sm[:, o + 16:o + 80] = cnt.reshape(1, 64)
    s_idx = np.arange(128)[:, None]
    t_idx = np.arange(128)[None, :]
    mask = (s_idx <= t_idx).astype(f)
    sm[:, o + 80:o + 80 + 512] = np.tile(mask, (1, 4))
    sm[:, o + 592:o + 592 + 128] = np.eye(128, dtype=f)
    out["smalls"] = sm
    bcs = []
    for l in layers:
        row = np.concatenate([inp["sgu_ln_g"][l].reshape(512), inp["sgu_ln_b"][l].reshape(512),
                              inp["sgu_b"][l].reshape(512)])
        bcs.append(np.broadcast_to(row[None, :], (128, 1536)))
    out["bc"] = np.ascontiguousarray(np.stack(bcs)).astype(f)
    out["sgw"] = np.stack([np.ascontiguousarray(inp["sgu_w"][l].transpose(2, 0, 1)).reshape(128, 512) for l in layers])
    out["plw"] = np.stack([np.ascontiguousarray(inp["pool_w"][l].transpose(1, 0, 2)).reshape(128, 512) for l in layers])
    return out


_CACHE = {}


def _get_prog(L, T, final_norm):
    key = (L, T, final_norm)
    if key not in _CACHE:
        _CACHE[key] = build_program(L, T, final_norm)
    return _CACHE[key]


def run_layers(hT_list, pT_list, inp, layers, final_norm):
    n = len(hT_list)
    T = hT_list[0].shape[1]
    wts = prep_weights(inp, layers)
    nc = _get_prog(len(layers), T, final_norm)
    in_maps = []
    for c in range(n):
        m = dict(wts)
        m["xT"] = hT_list[c]
        m["pT"] = pT_list[c]
        in_maps.append(m)
    res = run_bass_kernel_spmd(nc, in_maps, core_ids=list(range(n)))
    return [r["outT"] for r in res.results]


FUSED = True


def kernel(**inputs):
    inp = {k: np.asarray(v) for k, v in inputs.items()}
    x = inp["x"]
    p = inp["p"]
    B = x.shape[0]
    hT = [np.ascontiguousarray(x[b].T) for b in range(B)]
    if FUSED:
        pT = [np.ascontiguousarray(p[:, b].transpose(0, 2, 1)) for b in range(B)]
        outs = run_layers(hT, pT, inp, list(range(DEPTH)), True)
    else:
        for l in range(DEPTH):
            pT = [np.ascontiguousarray(p[l:l + 1, b].transpose(0, 2, 1)) for b in range(B)]
            hT = run_layers(hT, pT, inp, [l], l == DEPTH - 1)
        outs = hT
    return np.stack([np.ascontiguousarray(o.T) for o in outs]).astype(np.float32)
```

```python
import numpy as np
import concourse.bass as bass
import concourse.mybir as mybir
from concourse.bass_utils import run_bass_kernel_spmd

F32 = mybir.dt.float32
BF16 = mybir.dt.bfloat16
AF = mybir.ActivationFunctionType
ALU = mybir.AluOpType

D = 2048
KD = D // 128
FF = 5632
KF = FF // 128
PLE = 256
DEPTH = 4
SEQ = 4096
BATCH = 8
TT = 512
KDVE = 17
NSLOT = 8
SLOT_ELEMS = 16 * 128
EPS = 1e-6
GELU_C = 0.7978845608028654
POOL_W = (2, 4, 8, 16)

SL = 200
O_GMIX, O_GFFN, O_GPLE, O_SCW, O_CFW, O_CFB, O_LNG, O_LNB, O_PSC = 0, 16, 32, 48, 60, 184, 188, 192, 196


def smalls_size(L):
    return L * SL + 16 + 64 + 512 + 128


import os as _os0
SIGALL = set(_os0.environ.get('SIGALL', '').split(','))


class Inst:
    __slots__ = ("eng", "fn", "dma", "deps", "signal", "sigval", "idx")

    def __init__(self, eng, fn, dma):
        self.eng = eng
        self.fn = fn
        self.dma = dma
        self.deps = []
        self.signal = False
        self.sigval = 0


class Res:
    __slots__ = ("writer", "readers", "dma_readers")

    def __init__(self):
        self.writer = None
        self.readers = {}
        self.dma_readers = []


class Prog:
    ENGS = ("pe", "act", "dve", "pool", "sp")

    def __init__(self):
        self.streams = {e: [] for e in self.ENGS}
        self.res = {}
        self.dma_keys = []

    def _r(self, key):
        r = self.res.get(key)
        if r is None:
            r = self.res[key] = Res()
        return r

    def op(self, eng, fn, reads=(), writes=(), dma=None):
        inst = Inst(eng, fn, dma)
        if dma is not None and dma not in self.dma_keys:
            self.dma_keys.append(dma)
        deps = {}

        def add(d, raw):
            if d is None or d is inst:
                return
            if d.dma is None and inst.dma is None and d.eng == eng:
                if not raw or eng == "pe":
                    return
            if d.dma is not None:
                deps[id(d)] = d
            else:
                k = ("e", d.eng)
                o = deps.get(k)
                if o is None or o.idx < d.idx:
                    deps[k] = d

        for key in reads:
            add(self._r(key).writer, True)
        for key in writes:
            r = self._r(key)
            add(r.writer, False)
            for d in r.readers.values():
                add(d, False)
            for d in r.dma_readers:
                add(d, False)
        inst.idx = len(self.streams[eng])
        if eng in SIGALL:
            inst.signal = True
        self.streams[eng].append(inst)
        inst.deps = list(deps.values())
        for d in inst.deps:
            d.signal = True
        for key in reads:
            r = self._r(key)
            if dma is not None:
                r.dma_readers.append(inst)
            else:
                r.readers[eng] = inst
        for key in writes:
            r = self._r(key)
            r.writer = inst
            r.readers = {}
            r.dma_readers = []
        return inst

    def emit(self, nc, engines, sems, dma_sems):
        cnt = {e: 0 for e in self.ENGS}
        dcnt = {k: 0 for k in self.dma_keys}
        for e in self.ENGS:
            for inst in self.streams[e]:
                if inst.dma is not None:
                    dcnt[inst.dma] += 16
                    inst.sigval = dcnt[inst.dma]
                elif inst.signal:
                    cnt[e] += 1
                    inst.sigval = cnt[e]
        for e in self.ENGS:
            eng = engines[e]
            waited = {}
            for inst in self.streams[e]:
                need = {}
                for d in inst.deps:
                    k = ("d", d.dma) if d.dma is not None else ("e", d.eng)
                    if need.get(k, 0) < d.sigval:
                        need[k] = d.sigval
                for k, v in need.items():
                    if waited.get(k, 0) >= v:
                        continue
                    waited[k] = v
                    sem = dma_sems[k[1]] if k[0] == "d" else sems[k[1]]
                    eng.wait_ge(sem, v)
                bi = inst.fn(eng)
                if inst.dma is not None:
                    bi.then_inc(dma_sems[inst.dma], 16)
                elif inst.signal:
                    bi.then_inc(sems[e], 1)


def build_program(L, T, final_norm):
    NT = T // TT
    nc = bass.Bass("TRN2", target_bir_lowering=False)
    NS = smalls_size(L)
    O_FG = L * SL
    O_CNT = O_FG + 16
    O_MASK = O_CNT + 64
    O_ID = O_MASK + 512

    xT = nc.dram_tensor("xT", [D, T], F32, kind="ExternalInput").ap()
    pT = nc.dram_tensor("pT", [L, PLE, T], F32, kind="ExternalInput").ap()
    w_in_d = nc.dram_tensor("w_in", [L, 32, 128, 16, 128], F32, kind="ExternalInput").ap()
    w_out_d = nc.dram_tensor("w_out", [L, 16, 128, 16, 128], F32, kind="ExternalInput").ap()
    w_gu_d = nc.dram_tensor("w_gu", [L, 88, 128, 16, 128], F32, kind="ExternalInput").ap()
    w_dn_d = nc.dram_tensor("w_dn", [L, 16, 128, 44, 128], F32, kind="ExternalInput").ap()
    w_pg_d = nc.dram_tensor("w_pg", [L, 16, 128, 16, 128], F32, kind="ExternalInput").ap()
    w_pp_d = nc.dram_tensor("w_pp", [L, 4, 128, 2, 512], F32, kind="ExternalInput").ap()
    smalls_d = nc.dram_tensor("smalls", [128, NS], F32, kind="ExternalInput").ap()
    bc_d = nc.dram_tensor("bc", [L, 128, 3 * 512], F32, kind="ExternalInput").ap()
    sgw_d = nc.dram_tensor("sgw", [L, 128, 512], F32, kind="ExternalInput").ap()
    plw_d = nc.dram_tensor("plw", [L, 128, 512], F32, kind="ExternalInput").ap()
    outT = nc.dram_tensor("outT", [D, T], F32, kind="ExternalOutput").ap()
    import os
    DBG = bool(os.environ.get("KDBG"))
    if DBG:
        dbg = nc.dram_tensor("dbg", [128, 8, 542], F32, kind="ExternalOutput").ap()
        dbgw = nc.dram_tensor("dbgw", [128, 4, 2048], BF16, kind="ExternalOutput").ap()

    import os as _os
    P = Prog()
    from contextlib import ExitStack
    es = ExitStack()

    def sb(name, shape, dt):
        return es.enter_context(nc.sbuf_tensor("sb_" + name, shape, dt))

    h = sb("h", [128, KD, TT], F32)
    y = sb("y", [128, KD, TT], BF16)
    abig = sb("abig", [128, KF, TT], BF16)
    ring = sb("ring", [128, NSLOT, SLOT_ELEMS], BF16)
    pTb = sb("pTb", [128, 2, TT], BF16)
    smalls = sb("smalls", [128, NS], F32)
    bc = sb("bcs", [128, 3, 512], F32)
    sgw_f = sb("sgw_f", [128, 512], F32)
    sgw_b = sb("sgw_b", [128, 512], BF16)
    plw_b = sb("plw_b", [128, 512], BF16)
    ones_b = sb("ones_b", [128, 128], BF16)
    ones_c = sb("ones_c", [128, 128], BF16)
    epsT = sb("epsT", [128, 1], F32)
    lnscr = sb("lnscr", [128, 2], F32)
    stB = sb("stB", [128, L, 4, 2], F32)
    stC = sb("stC", [128, L, 4, 30], F32)
    stD = sb("stD", [128, L, 4, 16], F32)
    NSQ = 5
    sqb = sb("sqb", [128, NSQ, TT], BF16)
    rs = sb("rs", [128, TT], F32)
    NTMP = 6
    tmpf = sb("tmpf", [128, NTMP, TT], F32)
    g4 = [abig[:, 32 + 2 * tc_:34 + 2 * tc_, :].rearrange("p a b -> p (a b)").bitcast(F32) for tc_ in range(4)]
    st6 = sb("st6", [128, 16, 6], F32)
    mv = sb("mv", [128, 16, 2], F32)
    nr = sb("nr", [128, 4, 16], F32)
    vn = sb("vn", [128, 4, 512], BF16)
    u2 = sb("u2", [128, 2, TT], F32)
    qB = sb("qB", [128, 2, TT + 2], F32)
    accB = sb("accB", [128, 2, TT], F32)
    qC = sb("qC", [128, 4, TT + 32], BF16)
    convC = sb("convC", [128, 4, TT], F32)
    accC_l = [abig[:, 40 + 2 * i_:42 + 2 * i_, :].rearrange("p a b -> p (a b)").bitcast(F32) for i_ in range(2)]
    msb = sb("msb", [128, TT], F32)
    zD = sb("zD", [128, 2, TT + 16], F32)
    dA = sb("dA", [128, TT + 16], F32)
    dB = sb("dB", [128, TT + 16], F32)
    plD = sb("plD", [128, 4, TT], BF16)
    ps = [es.enter_context(nc.psum_tensor(f"ps{i}", [128, 512], F32)) for i in range(8)]

    ctr = {"bank": int(_os.environ.get("BANK0", "0")), "tmp": 0, "sq": 0}

    def next_bank():
        b = ctr["bank"]
        ctr["bank"] = (b + 1) % 7
        return b

    def next_tmp():
        i = ctr["tmp"]
        ctr["tmp"] = (i + 1) % NTMP
        return i

    def next_sq():
        i = ctr["sq"]
        ctr["sq"] = (i + 1) % NSQ
        return i

    units = []

    def build_units():
        for t in range(NT):
            for l in range(L):
                for u in range(32):
                    units.append((w_in_d[l, u], 16, 128))
                for u in range(16):
                    units.append((w_out_d[l, u], 16, 128))
                for u in range(88):
                    units.append((w_gu_d[l, u], 16, 128))
                for m in range(16):
                    units.append((w_dn_d[l, m, :, 0:16, :], 16, 128))
                    units.append((w_dn_d[l, m, :, 16:32, :], 16, 128))
                    units.append((w_dn_d[l, m, :, 32:44, :], 12, 128))
                for q in range(4):
                    for mm in range(4):
                        units.append((w_pg_d[l, q * 4 + mm], 16, 128))
                    units.append((w_pp_d[l, q], 2, 512))

    build_units()
    RING0 = int(_os.environ.get('RING0', '0'))
    ust = {"loaded": 0, "acq": 0, "rel": 0}

    def slot_view(slot, kc, ncols):
        return ring[:, slot, 0:kc * ncols].rearrange("p (k c) -> p k c", k=kc)

    def emit_loads():
        while ust["loaded"] < len(units) and ust["loaded"] < ust["rel"] + NSLOT:
            i = ust["loaded"]
            src, kc, ncols = units[i]
            slot = (i + RING0) % NSLOT
            dst = slot_view(slot, kc, ncols)
            P.op("pool", lambda g, dst=dst, src=src: g.dma_start(out=dst, in_=src),
                 writes=[("ring", slot)], dma=("ring", slot))
            ust["loaded"] += 1

    def acquire():
        i = ust["acq"]
        ust["acq"] += 1
        assert i < ust["loaded"], "unit not loaded (ring deadlock)"
        src, kc, ncols = units[i]
        slot = (i + RING0) % NSLOT
        return slot, slot_view(slot, kc, ncols), kc

    def release(n=1):
        ust["rel"] += n
        emit_loads()

    def S(col, n=1):
        return smalls[:, col:col + n]

    def gelu2(bank, out_ap, out_key):
        ix, isq = next_tmp(), next_tmp()
        xs, sq = tmpf[:, ix, :], tmpf[:, isq, :]
        P.op("act", lambda a: a.activation(out=xs, in_=ps[bank][:], func=AF.Copy),
             reads=[("ps", bank)], writes=[("tmp", ix)])
        P.op("act", lambda a: a.activation(out=sq, in_=ps[bank][:], func=AF.Square, scale=0.044715 ** 0.5),
             reads=[("ps", bank)], writes=[("tmp", isq)])
        P.op("dve", lambda v: v.scalar_tensor_tensor(out=sq, in0=sq, scalar=1.0, in1=xs, op0=ALU.add, op1=ALU.mult),
             reads=[("tmp", isq), ("tmp", ix)], writes=[("tmp", isq)])
        P.op("act", lambda a: a.activation(out=sq, in_=sq, func=AF.Tanh, scale=GELU_C),
             reads=[("tmp", isq)], writes=[("tmp", isq)])
        P.op("dve", lambda v: v.scalar_tensor_tensor(out=out_ap, in0=sq, scalar=1.0, in1=xs,
                                                     op0=ALU.add, op1=ALU.mult),
             reads=[("tmp", isq), ("tmp", ix)], writes=[out_key])

    def rsqrt_big(bank_or_ap, src_keys, scale=1.0):
        P.op("act", lambda a: a.activation(out=rs[:], in_=bank_or_ap, func=AF.Ln,
                                           bias=epsT[:, 0:1], scale=scale),
             reads=list(src_keys) + [("eps",)], writes=[("rs",)])
        P.op("act", lambda a: a.activation(out=rs[:], in_=rs[:], func=AF.Exp, scale=-0.5),
             reads=[("rs",)], writes=[("rs",)])

    SB = 7
    stat_state = {"pend": []}

    def stat_act(k):
        i = next_sq()
        P.op("act", lambda a, k=k, i=i: a.activation(out=sqb[:, i, :], in_=h[:, k, :], func=AF.Square),
             reads=[("h", k)], writes=[("sq", i)])
        stat_state["pend"].append((k, i))

    def stat_flush():
        for k, i in stat_state["pend"]:
            P.op("pe", lambda t_, k=k, i=i: t_.matmul(ps[SB][:], lhsT=ones_b[:], rhs=sqb[:, i, :],
                                                      start=(k == 0), stop=(k == KD - 1)),
                 reads=[("sq", i), ("ones_b",)], writes=[("ps", SB)])
        stat_state["pend"] = []

    def ln_preload():
        P.op("act", lambda a: a.activation(out=lnscr[:, 0:1], in_=epsT[:, 0:1], func=AF.Ln),
             reads=[("eps",)], writes=[("lnscr",)])

    def norm_finish(gcol, dst_h=False):
        stat_flush()
        rsqrt_big(ps[SB][:], [("ps", SB)])
        for k in range(KD):
            if dst_h:
                P.op("dve", lambda v, k=k: v.scalar_tensor_tensor(out=h[:, k, :], in0=h[:, k, :],
                                                                  scalar=S(gcol + k), in1=rs[:],
                                                                  op0=ALU.mult, op1=ALU.mult),
                     reads=[("h", k), ("rs",), ("smalls",)], writes=[("h", k)])
            else:
                P.op("dve", lambda v, k=k: v.scalar_tensor_tensor(out=y[:, k, :], in0=h[:, k, :],
                                                                  scalar=S(gcol + k), in1=rs[:],
                                                                  op0=ALU.mult, op1=ALU.mult),
                     reads=[("h", k), ("rs",), ("smalls",)], writes=[("y", k)])

    def mm_unit(bank, uview, kc, rhs_of_k, rhs_keys, slot, first=True, last=True, k0=0, korder=None):
        ks = list(korder) if korder is not None else list(range(kc))
        for n_, k in enumerate(ks):
            P.op("pe", lambda t, k=k, n_=n_: t.matmul(ps[bank][:], lhsT=uview[:, k, :], rhs=rhs_of_k(k0 + k),
                                                      start=(first and n_ == 0), stop=(last and n_ == kc - 1)),
                 reads=[("ring", slot)] + [rk(k0 + k) for rk in rhs_keys], writes=[("ps", bank)])

    def y_group(dbgi=None):
        slot, uv, kc = acquire()
        bank = next_bank()
        mm_unit(bank, uv, kc, lambda k: y[:, k, :], [lambda k: ("y", k)], slot)
        if dbgi is not None:
            P.op("sp", lambda s_: s_.dma_start(out=dbgw[:, dbgi, :], in_=ring[:, slot, :]),
                 reads=[("ring", slot)], writes=[("dbgw", dbgi)], dma=("dbgw",))
        release()
        return bank

    def y_groups_kmajor(n):
        us = [acquire() for _ in range(n)]
        banks = [next_bank() for _ in range(n)]
        for k in range(KD):
            for (slot, uv, kc), bank in zip(us, banks):
                P.op("pe", lambda t_, k=k, uv=uv, bank=bank: t_.matmul(ps[bank][:], lhsT=uv[:, k, :], rhs=y[:, k, :],
                                                                       start=(k == 0), stop=(k == KD - 1)),
                     reads=[("ring", slot), ("y", k)], writes=[("ps", bank)])
        release(n)
        return banks

    P.op("sp", lambda s: s.dma_start(out=smalls[:], in_=smalls_d), writes=[("smalls",)], dma=("smalls",))
    P.op("dve", lambda v: v.memset(ones_b[:], 1.0 / D), writes=[("ones_b",)])
    P.op("dve", lambda v: v.memset(ones_c[:], 1.0 / 512), writes=[("ones_c",)])
    P.op("dve", lambda v: v.memset(epsT[:], EPS), writes=[("eps",)])
    P.op("dve", lambda v: v.memset(stB[:], 0.0), writes=[("stB", l_, c_) for l_ in range(L) for c_ in range(4)])
    P.op("dve", lambda v: v.memset(stC[:], 0.0), writes=[("stC", l_, c_) for l_ in range(L) for c_ in range(4)])
    P.op("dve", lambda v: v.memset(stD[:], 0.0), writes=[("stD", l_, c_) for l_ in range(L) for c_ in range(4)])
    for l in range(L):
        c0 = l * SL + O_CFW
        P.op("dve", lambda v, c0=c0: v.tensor_scalar(out=smalls[:, c0:c0 + 124], in0=smalls[:, c0:c0 + 124],
                                                     scalar1=0.5, scalar2=None, op0=ALU.mult),
             reads=[("smalls",)], writes=[("smalls",)])
    emit_loads()

    xTv = xT.rearrange("(k p) t -> p k t", p=128)
    oTv = outT.rearrange("(k p) t -> p k t", p=128)


    def diag_ap(c_, k):
        i_ = c_ * (31 - KDVE) + (k - KDVE)
        j = 16 + i_ // 4
        return abig[:, j, (i_ % 4) * 128:(i_ % 4 + 1) * 128], ("a", j)

    def diag_build_thunks(l_):
        th_ = []
        for c_ in range(4):
            wc = l_ * SL + O_CFW + c_ * 31
            for k in range(KDVE, 31):
                dap, dkey = diag_ap(c_, k)
                th_.append(lambda dap=dap, dkey=dkey, col=wc + k: P.op(
                    "act", lambda a: a.activation(out=dap, in_=smalls[:, O_ID:O_ID + 128], func=AF.Identity,
                                                  scale=S(col)),
                    reads=[("smalls",)], writes=[dkey]))
        return th_

    for th_ in diag_build_thunks(0):
        th_()
    pending = []

    for t in range(NT):
        tok0 = t * TT
        for kq in range(4):
            P.op("sp", lambda s, kq=kq, tok0=tok0: s.dma_start(out=h[:, kq * 4:(kq + 1) * 4, :],
                                                    in_=xTv[:, kq * 4:(kq + 1) * 4, tok0:tok0 + TT]),
                 writes=[("h", k) for k in range(kq * 4, kq * 4 + 4)], dma=("x", kq))
        for k in range(KD):
            stat_act(k)
            stat_flush()
        for l in range(L):
            so = l * SL
            P.op("sp", lambda s, l=l: s.dma_start(out=bc[:].rearrange("p a b -> p (a b)"), in_=bc_d[l]),
                 writes=[("bc",)], dma=("bc",))
            P.op("sp", lambda s, l=l: s.dma_start(out=sgw_f[:], in_=sgw_d[l]), writes=[("sgw_f",)], dma=("sgw",))
            P.op("dve", lambda v: v.tensor_tensor(out=sgw_b[:], in0=sgw_f[:], in1=smalls[:, O_MASK:O_MASK + 512],
                                                  op=ALU.mult),
                 reads=[("sgw_f",), ("smalls",)], writes=[("sgw_b",)])
            P.op("pool", lambda g, l=l: g.dma_start(out=plw_b[:], in_=plw_d[l]), writes=[("plw_b",)], dma=("plw",))
            P.op("pool", lambda g, l=l, tok0=tok0: g.dma_start(
                out=pTb[:], in_=pT[l].rearrange("(k p) t -> p k t", p=128)[:, :, tok0:tok0 + TT]),
                writes=[("pTb",)], dma=("pTb",))

            norm_finish(so + O_GMIX)

            while pending:
                pending.pop(0)()
            vs = [acquire() for _ in range(4)]
            s0 = vs[0][0]
            assert [s_[0] for s_ in vs] == [s0, s0 + 1, s0 + 2, s0 + 3] and s0 + 3 < NSLOT
            vbanks = [next_bank() for _ in range(4)]
            for k in range(KD):
                for tc in range(4):
                    P.op("pe", lambda tt, k=k, tc=tc, bank=vbanks[tc], s0=s0: tt.matmul(
                        ps[bank][:].rearrange("p (a b) -> p a b", a=4),
                        lhsT=y[:, k, tc * 128:(tc + 1) * 128],
                        rhs=ring[:, s0:s0 + 4, k * 128:(k + 1) * 128],
                        start=(k == 0), stop=(k == KD - 1)),
                        reads=[("ring", s0), ("ring", s0 + 1), ("ring", s0 + 2), ("ring", s0 + 3), ("y", k)],
                        writes=[("ps", vbanks[tc])])
            for tc in range(4):
                bank = vbanks[tc]
                gelu2(bank, g4[tc], ("g4", tc))
                for hd in range(4):
                    j = tc * 4 + hd
                    P.op("dve", lambda v, tc=tc, hd=hd, j=j: v.bn_stats(out=st6[:, j, :],
                                                                        in_=g4[tc][:, hd * 128:(hd + 1) * 128]),
                         reads=[("g4", tc)], writes=[("st6", j)])
                for hd in range(4):
                    j = tc * 4 + hd
                    P.op("dve", lambda v, j=j: v.bn_aggr(out=mv[:, j, :], in_=st6[:, j, :]),
                         reads=[("st6", j)], writes=[("mv",)])
            release(4)
            for c in range(4):
                zi = c % 2
                w = POOL_W[c]
                bz = y_group()
                P.op("act", lambda a, zi=zi, l=l, c=c: a.activation(out=zD[:, zi, 0:16], in_=stD[:, l, c, :],
                                                                     func=AF.Copy),
                     reads=[("stD", l, c)], writes=[("zDh", zi)])
                P.op("act", lambda a, zi=zi, bz=bz: a.activation(out=zD[:, zi, 16:16 + TT], in_=ps[bz][:],
                                                                 func=AF.Copy),
                     reads=[("ps", bz)], writes=[("zD", zi)])
                P.op("act", lambda a, zi=zi, l=l, c=c: a.activation(out=stD[:, l, c, :], in_=zD[:, zi, TT:TT + 16],
                                                                     func=AF.Copy),
                     reads=[("zD", zi)], writes=[("stD", l, c)])
                cur, cur_key, lo = zD[:, zi, :], [("zD", zi), ("zDh", zi)], 0
                bufs = [(dA, ("dA",)), (dB, ("dB",))]
                for j in range(c + 1):
                    sh = 1 << j
                    nb_, nk = bufs[j % 2]
                    nlo = lo + sh
                    P.op("dve", lambda v, cur=cur, nb_=nb_, nlo=nlo, sh=sh: v.tensor_tensor(
                        out=nb_[:, nlo:TT + 16], in0=cur[:, nlo:TT + 16], in1=cur[:, nlo - sh:TT + 16 - sh],
                        op=ALU.add),
                        reads=list(cur_key), writes=[nk])
                    cur, cur_key, lo = nb_[:], [nk], nlo
                P.op("dve", lambda v, cur=cur, zi=zi, w=w, c=c: v.scalar_tensor_tensor(
                    out=plD[:, c, :], in0=cur[:, 16:16 + TT], scalar=1.0 / w, in1=zD[:, zi, 16:16 + TT],
                    op0=ALU.mult, op1=ALU.subtract),
                    reads=list(cur_key) + [("zD", zi)], writes=[("plD", c)])
                if t == 0:
                    i16 = next_tmp()
                    t16 = tmpf[:, i16, 0:16]
                    P.op("dve", lambda v, cur=cur, t16=t16, c=c: v.tensor_tensor(
                        out=t16, in0=cur[:, 16:32], in1=smalls[:, O_CNT + c * 16:O_CNT + c * 16 + 16], op=ALU.mult),
                        reads=list(cur_key) + [("smalls",)], writes=[("tmp", i16)])
                    P.op("dve", lambda v, t16=t16, zi=zi, c=c: v.tensor_tensor(
                        out=plD[:, c, 0:16], in0=t16, in1=zD[:, zi, 16:32], op=ALU.subtract),
                        reads=[("tmp", i16), ("zD", zi)], writes=[("plD", c)])

            def conv_c(c):
                bcv = next_bank()
                for k in range(KDVE, 31):
                    dap, dkey = diag_ap(c, k)
                    P.op("pe", lambda tt, c=c, k=k, dap=dap, bcv=bcv: tt.matmul(
                        ps[bcv][:], lhsT=dap, rhs=qC[:, c, k:k + TT], start=(k == KDVE), stop=(k == 30)),
                        reads=[dkey, ("qC", c), ("qCh", c)], writes=[("ps", bcv)])
                P.op("dve", lambda v, c=c, bcv=bcv, so=so: v.scalar_tensor_tensor(
                    out=convC[:, c, :], in0=ps[bcv][:], scalar=S(so + O_CFB + c), in1=convC[:, c, :],
                    op0=ALU.add, op1=ALU.add),
                    reads=[("ps", bcv), ("convC", c), ("smalls",)], writes=[("convC", c)])
                P.op("dve", lambda v, c=c: v.tensor_tensor(out=convC[:, c, :], in0=convC[:, c, :], in1=accC_l[c % 2],
                                                           op=ALU.add),
                     reads=[("convC", c), ("accC", c % 2)], writes=[("convC", c)])

            def conv_dve(c):
                wc = so + O_CFW + c * 31
                ai = c % 2
                P.op("dve", lambda v, c=c, wc=wc: v.tensor_scalar(
                    out=convC[:, c, :], in0=qC[:, c, 0:TT], scalar1=S(wc), scalar2=None, op0=ALU.mult),
                    reads=[("qC", c), ("qCh", c), ("smalls",)], writes=[("convC", c)])
                P.op("dve", lambda v, c=c, wc=wc, ai=ai: v.tensor_scalar(
                    out=accC_l[ai], in0=qC[:, c, 1:1 + TT], scalar1=S(wc + 1), scalar2=None, op0=ALU.mult),
                    reads=[("qC", c), ("qCh", c), ("smalls",)], writes=[("accC", ai)])
                for kk in range(2, KDVE):
                    if kk % 2 == 0:
                        P.op("dve", lambda v, c=c, wc=wc, kk=kk: v.scalar_tensor_tensor(
                            out=convC[:, c, :], in0=qC[:, c, kk:kk + TT], scalar=S(wc + kk), in1=convC[:, c, :],
                            op0=ALU.mult, op1=ALU.add),
                            reads=[("qC", c), ("qCh", c), ("convC", c), ("smalls",)], writes=[("convC", c)])
                    else:
                        P.op("dve", lambda v, c=c, wc=wc, kk=kk, ai=ai: v.scalar_tensor_tensor(
                            out=accC_l[ai], in0=qC[:, c, kk:kk + TT], scalar=S(wc + kk), in1=accC_l[ai],
                            op0=ALU.mult, op1=ALU.add),
                            reads=[("qC", c), ("qCh", c), ("accC", ai), ("smalls",)], writes=[("accC", ai)])

            for c in range(4):
                bg_ = y_group()
                ig = next_tmp()
                th = tmpf[:, ig, :]
                P.op("act", lambda a, bg_=bg_, th=th: a.activation(out=th, in_=ps[bg_][:], func=AF.Tanh, scale=0.5),
                     reads=[("ps", bg_)], writes=[("tmp", ig)])
                ba = y_group()
                P.op("act", lambda a, l=l, c=c: a.activation(out=qC[:, c, 0:30], in_=stC[:, l, c, :], func=AF.Copy),
                     reads=[("stC", l, c)], writes=[("qCh", c)])
                P.op("dve", lambda v, c=c, ba=ba, th=th: v.scalar_tensor_tensor(
                    out=qC[:, c, 30:30 + TT], in0=th, scalar=1.0, in1=ps[ba][:], op0=ALU.add, op1=ALU.mult),
                    reads=[("ps", ba), ("tmp", ig)], writes=[("qC", c)])
                P.op("act", lambda a, l=l, c=c: a.activation(out=stC[:, l, c, :], in_=qC[:, c, TT:TT + 30],
                                                             func=AF.Copy),
                     reads=[("qC", c)], writes=[("stC", l, c)])
                conv_dve(c)
                if c >= 1:
                    conv_c(c - 1)

            nx, ny, na, nb = nr[:, 0, :], nr[:, 1, :], nr[:, 2, :], nr[:, 3, :]
            P.op("dve", lambda v: v.tensor_scalar(out=nx, in0=mv[:, :, 1], scalar1=4.0 * EPS, scalar2=None,
                                                  op0=ALU.add), reads=[("mv",)], writes=[("nx",)])
            P.op("dve", lambda v: v.tensor_scalar(out=na, in0=nx, scalar1=0.5, scalar2=0.5, op0=ALU.mult,
                                                  op1=ALU.add), reads=[("nx",)], writes=[("na",)])
            P.op("dve", lambda v: v.reciprocal(out=ny, in_=na), reads=[("na",)], writes=[("ny",)])
            for it in range(6):
                P.op("dve", lambda v: v.tensor_tensor(out=na, in0=ny, in1=ny, op=ALU.mult),
                     reads=[("ny",)], writes=[("na",)])
                P.op("dve", lambda v: v.scalar_tensor_tensor(out=nb, in0=na, scalar=-0.5, in1=nx,
                                                             op0=ALU.mult, op1=ALU.mult),
                     reads=[("na",), ("nx",)], writes=[("nb",)])
                P.op("dve", lambda v: v.scalar_tensor_tensor(out=ny, in0=nb, scalar=1.5, in1=ny,
                                                             op0=ALU.add, op1=ALU.mult),
                     reads=[("nb",), ("ny",)], writes=[("ny",)])
            for tc in range(4):
                it_ = next_tmp()
                tv = tmpf[:, it_, :]
                for hd in range(4):
                    j = tc * 4 + hd
                    P.op("dve", lambda v, tc=tc, hd=hd, j=j, tv=tv: v.tensor_scalar(
                        out=tv[:, hd * 128:(hd + 1) * 128], in0=g4[tc][:, hd * 128:(hd + 1) * 128],
                        scalar1=mv[:, j, 0:1], scalar2=nr[:, 1, j:j + 1], op0=ALU.subtract, op1=ALU.mult),
                        reads=[("g4", tc), ("mv",), ("ny",)], writes=[("tmp", it_)])
                P.op("dve", lambda v, tv=tv: v.tensor_tensor(out=tv, in0=tv, in1=bc[:, 0, :], op=ALU.mult),
                     reads=[("tmp", it_), ("bc",)], writes=[("tmp", it_)])
                P.op("dve", lambda v, tv=tv, tc=tc: v.tensor_tensor(out=vn[:, tc, :], in0=tv, in1=bc[:, 1, :],
                                                                    op=ALU.add),
                     reads=[("tmp", it_), ("bc",)], writes=[("vn", tc)])

            for c in range(4):
                qi = c % 2
                bh = y_group()
                if c == 0:
                    conv_c(3)
                ih = next_tmp()
                hs = tmpf[:, ih, :]
                P.op("act", lambda a, bh=bh, hs=hs: a.activation(out=hs, in_=ps[bh][:], func=AF.Copy),
                     reads=[("ps", bh)], writes=[("tmp", ih)])
                bcg = y_group()
                P.op("act", lambda a, qi=qi, l=l, c=c: a.activation(out=qB[:, qi, 0:2], in_=stB[:, l, c, :],
                                                                     func=AF.Copy),
                     reads=[("stB", l, c)], writes=[("qBh", qi)])
                P.op("dve", lambda v, qi=qi, bcg=bcg, hs=hs: v.tensor_tensor(out=qB[:, qi, 2:2 + TT], in0=ps[bcg][:],
                                                                             in1=hs, op=ALU.mult),
                     reads=[("ps", bcg), ("tmp", ih)], writes=[("qB", qi)])
                P.op("act", lambda a, qi=qi, l=l, c=c: a.activation(out=stB[:, l, c, :], in_=qB[:, qi, TT:TT + 2],
                                                                     func=AF.Copy),
                     reads=[("qB", qi)], writes=[("stB", l, c)])
                wc = so + O_SCW + c * 3
                P.op("dve", lambda v, qi=qi, wc=wc: v.tensor_scalar(out=accB[:, qi, :], in0=qB[:, qi, 0:TT],
                                                                    scalar1=S(wc), scalar2=None, op0=ALU.mult),
                     reads=[("qB", qi), ("qBh", qi), ("smalls",)], writes=[("accB", qi)])
                for kk in (1, 2):
                    P.op("dve", lambda v, qi=qi, wc=wc, kk=kk: v.scalar_tensor_tensor(
                        out=accB[:, qi, :], in0=qB[:, qi, kk:kk + TT], scalar=S(wc + kk), in1=accB[:, qi, :],
                        op0=ALU.mult, op1=ALU.add),
                        reads=[("qB", qi), ("qBh", qi), ("accB", qi), ("smalls",)], writes=[("accB", qi)])
                bbg = y_group()
                P.op("dve", lambda v, qi=qi, bbg=bbg, c=c: v.tensor_tensor(out=abig[:, 4 + c, :], in0=ps[bbg][:],
                                                                           in1=accB[:, qi, :], op=ALU.mult),
                     reads=[("ps", bbg), ("accB", qi)], writes=[("a", 4 + c)])

            b1, b2 = next_bank(), next_bank()
            for c in range(4):
                isq = next_tmp()
                icb = next_tmp()
                sq = tmpf[:, isq, :].bitcast(BF16)[:, 0:TT]
                cb = tmpf[:, icb, :].bitcast(BF16)[:, 0:TT]
                P.op("act", lambda a, c=c, cb=cb: a.activation(out=cb, in_=convC[:, c, :], func=AF.Copy),
                     reads=[("convC", c)], writes=[("tmp", icb)])
                P.op("act", lambda a, c=c, sq=sq: a.activation(out=sq, in_=convC[:, c, :], func=AF.Square),
                     reads=[("convC", c)], writes=[("tmp", isq)])
                P.op("pe", lambda tt, c=c, b1=b1, cb=cb: tt.matmul(ps[b1][:], lhsT=ones_c[:], rhs=cb,
                                                                   start=(c == 0), stop=(c == 3)),
                     reads=[("tmp", icb), ("ones_c",)], writes=[("ps", b1)])
                P.op("pe", lambda tt, c=c, sq=sq, b2=b2: tt.matmul(ps[b2][:], lhsT=ones_c[:], rhs=sq,
                                                                   start=(c == 0), stop=(c == 3)),
                     reads=[("tmp", isq), ("ones_c",)], writes=[("ps", b2)])
            iv = next_tmp()
            var = tmpf[:, iv, :]
            P.op("act", lambda a, b1=b1: a.activation(out=msb[:], in_=ps[b1][:], func=AF.Copy),
                 reads=[("ps", b1)], writes=[("msb",)])
            P.op("dve", lambda v, var=var: v.tensor_tensor(out=var, in0=msb[:], in1=msb[:], op=ALU.mult),
                 reads=[("msb",)], writes=[("tmp", iv)])
            P.op("dve", lambda v, var=var, b2=b2: v.tensor_tensor(out=var, in0=ps[b2][:], in1=var, op=ALU.subtract),
                 reads=[("ps", b2), ("tmp", iv)], writes=[("tmp", iv)])
            rsqrt_big(var, [("tmp", iv)])
            for c in range(4):
                ix = next_tmp()
                xc = tmpf[:, ix, :]
                P.op("dve", lambda v, c=c, xc=xc: v.tensor_tensor(out=xc, in0=convC[:, c, :], in1=msb[:],
                                                                  op=ALU.subtract),
                     reads=[("convC", c), ("msb",)], writes=[("tmp", ix)])
                P.op("dve", lambda v, xc=xc: v.tensor_tensor(out=xc, in0=xc, in1=rs[:], op=ALU.mult),
                     reads=[("tmp", ix), ("rs",)], writes=[("tmp", ix)])
                P.op("act", lambda a, c=c, xc=xc, so=so: a.activation(out=abig[:, 8 + c, :], in_=xc, func=AF.Silu,
                                                               scale=S(so + O_LNG + c), bias=S(so + O_LNB + c)),
                     reads=[("tmp", ix), ("smalls",)], writes=[("a", 8 + c)])

            for hd in range(4):
                bank = y_group()
                ui = hd % 2
                gelu2(bank, u2[:, ui, :], ("u2", ui))
                bank2 = next_bank()
                for tc in range(4):
                    P.op("pe", lambda tt, tc=tc, hd=hd, bank2=bank2: tt.matmul(
                        ps[bank2][:, tc * 128:(tc + 1) * 128], lhsT=vn[:, tc, hd * 128:(hd + 1) * 128],
                        rhs=sgw_b[:, hd * 128:(hd + 1) * 128], start=True, stop=True),
                        reads=[("vn", tc), ("sgw_b",)], writes=[("ps", bank2)])
                it_ = next_tmp()
                tv = tmpf[:, it_, :]
                P.op("dve", lambda v, tv=tv, hd=hd, bank2=bank2: v.tensor_tensor(
                    out=tv.rearrange("p (a b) -> p a b", a=4),
                    in0=ps[bank2][:].rearrange("p (a b) -> p a b", a=4),
                    in1=bc[:, 2, hd * 128:(hd + 1) * 128].unsqueeze(1).broadcast_to([128, 4, 128]),
                    op=ALU.add),
                    reads=[("ps", bank2), ("bc",)], writes=[("tmp", it_)])
                P.op("dve", lambda v, tv=tv, hd=hd, ui=ui: v.scalar_tensor_tensor(
                    out=abig[:, hd, :], in0=tv, scalar=0.5, in1=u2[:, ui, :], op0=ALU.mult, op1=ALU.mult),
                    reads=[("tmp", it_), ("u2", ui)], writes=[("a", hd)])

            for c in range(4):
                bp = next_bank()
                P.op("pe", lambda tt, c=c, bp=bp: tt.matmul(ps[bp][:], lhsT=plw_b[:, c * 128:(c + 1) * 128],
                                                            rhs=plD[:, c, :], start=True, stop=True),
                     reads=[("plw_b",), ("plD", c)], writes=[("ps", bp)])
                P.op("act", lambda a, c=c, bp=bp, so=so: a.activation(out=abig[:, 12 + c, :], in_=ps[bp][:],
                                                                      func=AF.Identity, scale=S(so + O_PSC + c)),
                     reads=[("ps", bp), ("smalls",)], writes=[("a", 12 + c)])

            ln_preload()
            for m in range(KD):
                slot, uv, kc = acquire()
                bank = next_bank()
                mm_unit(bank, uv, kc, lambda k: abig[:, k, :], [lambda k: ("a", k)], slot,
                        korder=list(range(4, 16)) + list(range(4)))
                release()
                stat_flush()
                P.op("dve", lambda v, m=m, bank=bank: v.tensor_tensor(out=h[:, m, :], in0=ps[bank][:],
                                                                      in1=h[:, m, :], op=ALU.add),
                     reads=[("ps", bank), ("h", m)], writes=[("h", m)])
                stat_act(m)

            norm_finish(so + O_GFFN)
            ffn_banks = y_groups_kmajor(4)
            for j in range(KF):
                bgt = ffn_banks[2 * j] if j < 2 else y_group()
                isg = next_tmp()
                sg = tmpf[:, isg, :]
                P.op("act", lambda a, bgt=bgt, sg=sg: a.activation(out=sg, in_=ps[bgt][:], func=AF.Silu),
                     reads=[("ps", bgt)], writes=[("tmp", isg)])
                bup = ffn_banks[2 * j + 1] if j < 2 else y_group()
                P.op("dve", lambda v, j=j, bup=bup, sg=sg: v.tensor_tensor(out=abig[:, j, :], in0=ps[bup][:],
                                                                           in1=sg, op=ALU.mult),
                     reads=[("ps", bup), ("tmp", isg)], writes=[("a", j)])
            ln_preload()
            for m in range(KD):
                bank = next_bank()
                k0 = 0
                for part in range(3):
                    slot, uv, kc = acquire()
                    mm_unit(bank, uv, kc, lambda k: abig[:, k, :], [lambda k: ("a", k)], slot,
                            first=(part == 0), last=(part == 2), k0=k0)
                    k0 += kc
                    release()
                stat_flush()
                P.op("dve", lambda v, m=m, bank=bank: v.tensor_tensor(out=h[:, m, :], in0=ps[bank][:],
                                                                      in1=h[:, m, :], op=ALU.add),
                     reads=[("ps", bank), ("h", m)], writes=[("h", m)])
                stat_act(m)

            norm_finish(so + O_GPLE)
            pending = diag_build_thunks((l + 1) % L) if not (t == NT - 1 and l == L - 1) else []
            need_stats = (l < L - 1) or final_norm
            ple_banks = y_groups_kmajor(4)
            for q in range(4):
                for mm in range(4):
                    bgt = ple_banks[mm] if q == 0 else y_group()
                    P.op("act", lambda a, bgt=bgt, mm=mm: a.activation(out=g4[mm], in_=ps[bgt][:],
                                                                       func=AF.Tanh, scale=0.5),
                         reads=[("ps", bgt)], writes=[("g4", mm)])
                    for _ in range(4):
                        if pending:
                            pending.pop(0)()
                stat_flush()
                slot, uv, kc = acquire()
                for mm in range(4):
                    m = q * 4 + mm
                    bank = next_bank()
                    for kk in range(2):
                        P.op("pe", lambda tt, kk=kk, mm=mm, bank=bank, uv=uv: tt.matmul(
                            ps[bank][:], lhsT=uv[:, kk, mm * 128:(mm + 1) * 128], rhs=pTb[:, kk, :],
                            start=(kk == 0), stop=(kk == 1)),
                            reads=[("ring", slot), ("pTb",)], writes=[("ps", bank)])
                    itp = next_tmp()
                    tp = tmpf[:, itp, :]
                    P.op("dve", lambda v, mm=mm, bank=bank, tp=tp: v.scalar_tensor_tensor(
                        out=tp, in0=g4[mm], scalar=1.0, in1=ps[bank][:], op0=ALU.add, op1=ALU.mult),
                        reads=[("g4", mm), ("ps", bank)], writes=[("tmp", itp)])
                    P.op("dve", lambda v, m=m, tp=tp: v.scalar_tensor_tensor(
                        out=h[:, m, :], in0=tp, scalar=0.5, in1=h[:, m, :], op0=ALU.mult, op1=ALU.add),
                        reads=[("tmp", itp), ("h", m)], writes=[("h", m)])
                    if need_stats:
                        stat_act(m)
                release()

        if final_norm:
            norm_finish(O_FG, dst_h=True)
        for kq in range(4):
            P.op("sp", lambda s, kq=kq, tok0=tok0: s.dma_start(out=oTv[:, kq * 4:(kq + 1) * 4, tok0:tok0 + TT],
                                                    in_=h[:, kq * 4:(kq + 1) * 4, :]),
                 reads=[("h", k) for k in range(kq * 4, kq * 4 + 4)], writes=[("out", t, kq)], dma=("o", kq))
    P.op("sp", lambda s: s.nop(), reads=[("out", t, kq) for t in range(NT) for kq in range(4)])
    assert ust["acq"] == len(units) and ust["loaded"] == len(units)

    sem_stack = ExitStack()
    sems = {e: sem_stack.enter_context(nc.semaphore(f"s_{e}")) for e in Prog.ENGS}
    dma_sems = {k: sem_stack.enter_context(nc.semaphore("d_" + "_".join(str(x) for x in k))) for k in P.dma_keys}
    with nc.Block() as block:
        engines = {}

        @block.tensor
        def _(e):
            P.emit_one(nc, "pe", e, sems, dma_sems)

        @block.scalar
        def _(e):
            P.emit_one(nc, "act", e, sems, dma_sems)

        @block.vector
        def _(e):
            P.emit_one(nc, "dve", e, sems, dma_sems)

        @block.gpsimd
        def _(e):
            P.emit_one(nc, "pool", e, sems, dma_sems)

        @block.sync
        def _(e):
            P.emit_one(nc, "sp", e, sems, dma_sems)
    sem_stack.close()
    es.close()
    return nc


def _assign_sigvals(self):
    cnt = {e: 0 for e in self.ENGS}
    dcnt = {k: 0 for k in self.dma_keys}
    for e in self.ENGS:
        for inst in self.streams[e]:
            if inst.dma is not None:
                dcnt[inst.dma] += 16
                inst.sigval = dcnt[inst.dma]
            elif inst.signal:
                cnt[e] += 1
                inst.sigval = cnt[e]
    self._assigned = True


def _emit_one(self, nc, e, eng, sems, dma_sems):
    if not getattr(self, "_assigned", False):
        _assign_sigvals(self)
    waited = {}
    for inst in self.streams[e]:
        need = {}
        for d in inst.deps:
            k = ("d", d.dma) if d.dma is not None else ("e", d.eng)
            if need.get(k, 0) < d.sigval:
                need[k] = d.sigval
        for k, v in need.items():
            if waited.get(k, 0) >= v:
                continue
            waited[k] = v
            sem = dma_sems[k[1]] if k[0] == "d" else sems[k[1]]
            eng.wait_ge(sem, v)
        bi = inst.fn(eng)
        if inst.dma is not None:
            bi.then_inc(dma_sems[inst.dma], 16)
        elif inst.signal:
            bi.then_inc(sems[e], 1)


Prog.emit_one = _emit_one


def _units(w, cols):
    K = w.shape[0] // 128
    wv = w.reshape(K, 128, w.shape[1] // 128, 128)
    idx = np.asarray([c // 128 for c in cols])
    return np.ascontiguousarray(wv[:, :, idx, :].transpose(2, 1, 0, 3))


W_IN_COLS = ([512 + 128 * i for i in range(4)]
             + [3584 + 128 * c for c in range(4)]
             + [x for c in range(4) for x in (2560 + 512 + 128 * c, 2560 + 128 * c)]
             + [x for c in range(4) for x in (1024 + 128 * c, 1024 + 1024 + 128 * c, 1024 + 512 + 128 * c)]
             + [128 * i for i in range(4)])


def prep_weights(inp, layers):
    L = len(layers)
    f = np.float32
    out = {}
    out["w_in"] = np.stack([_units(inp["w_in"][l], W_IN_COLS) for l in layers])
    out["w_out"] = np.stack([_units(inp["w_out"][l], [128 * i for i in range(16)]) for l in layers])
    gu = []
    for l in layers:
        g = _units(inp["w_gate"][l], [128 * i for i in range(44)])
        u = _units(inp["w_up"][l], [128 * i for i in range(44)])
        gu.append(np.stack([g, u], axis=1).reshape(88, 128, 16, 128))
    out["w_gu"] = np.stack(gu)
    out["w_dn"] = np.stack([_units(inp["w_down"][l], [128 * i for i in range(16)]) for l in layers])
    out["w_pg"] = np.stack([_units(inp["w_ple_gate"][l], [128 * i for i in range(16)]) for l in layers])
    pp = []
    for l in layers:
        w = inp["w_ple_proj"][l].reshape(2, 128, 4, 512)
        pp.append(np.ascontiguousarray(w.transpose(2, 1, 0, 3)))
    out["w_pp"] = np.stack(pp)
    NS = smalls_size(L)
    sm = np.zeros((128, NS), f)

    def pk(v):
        return v.reshape(-1, 128).T

    for li, l in enumerate(layers):
        o = li * SL
        sm[:, o + O_GMIX:o + O_GMIX + 16] = pk(inp["norm_mix_g"][l])
        sm[:, o + O_GFFN:o + O_GFFN + 16] = pk(inp["norm_ffn_g"][l])
        sm[:, o + O_GPLE:o + O_GPLE + 16] = pk(inp["norm_ple_g"][l])
        sm[:, o + O_SCW:o + O_SCW + 12] = inp["sc_conv_w"][l].reshape(3, 4, 128).transpose(2, 1, 0).reshape(128, 12)
        sm[:, o + O_CFW:o + O_CFW + 124] = inp["cf_conv_w"][l].reshape(31, 4, 128).transpose(2, 1, 0).reshape(128, 124)
        sm[:, o + O_CFB:o + O_CFB + 4] = pk(inp["cf_conv_b"][l])
        sm[:, o + O_LNG:o + O_LNG + 4] = pk(inp["cf_ln_g"][l])
        sm[:, o + O_LNB:o + O_LNB + 4] = pk(inp["cf_ln_b"][l])
        sm[:, o + O_PSC:o + O_PSC + 4] = pk(inp["pool_scale"][l])
    o = L * SL
    sm[:, o:o + 16] = pk(inp["final_norm_g"])
    pos = np.arange(1, 17, dtype=f)
    cnt = np.stack([1.0 / np.minimum(pos, w) for w in POOL_W]).astype(f)
    sm[:, o + 16:o + 80] = cnt.reshape(1, 64)
    s_idx = np.arange(128)[:, None]
    t_idx = np.arange(128)[None, :]
    mask = (s_idx <= t_idx).astype(f)
    sm[:, o + 80:o + 80 + 512] = np.tile(mask, (1, 4))
    sm[:, o + 592:o + 592 + 128] = np.eye(128, dtype=f)
    out["smalls"] = sm
    bcs = []
    for l in layers:
        row = np.concatenate([inp["sgu_ln_g"][l].reshape(512), inp["sgu_ln_b"][l].reshape(512),
                              inp["sgu_b"][l].reshape(512)])
        bcs.append(np.broadcast_to(row[None, :], (128, 1536)))
    out["bc"] = np.ascontiguousarray(np.stack(bcs)).astype(f)
    out["sgw"] = np.stack([np.ascontiguousarray(inp["sgu_w"][l].transpose(2, 0, 1)).reshape(128, 512) for l in layers])
    out["plw"] = np.stack([np.ascontiguousarray(inp["pool_w"][l].transpose(1, 0, 2)).reshape(128, 512) for l in layers])
    return out


_CACHE = {}


def _get_prog(L, T, final_norm):
    key = (L, T, final_norm)
    if key not in _CACHE:
        _CACHE[key] = build_program(L, T, final_norm)
    return _CACHE[key]


def run_layers(hT_list, pT_list, inp, layers, final_norm):
    n = len(hT_list)
    T = hT_list[0].shape[1]
    wts = prep_weights(inp, layers)
    nc = _get_prog(len(layers), T, final_norm)
    in_maps = []
    for c in range(n):
        m = dict(wts)
        m["xT"] = hT_list[c]
        m["pT"] = pT_list[c]
        in_maps.append(m)
    res = run_bass_kernel_spmd(nc, in_maps, core_ids=list(range(n)))
    return [r["outT"] for r in res.results]


FUSED = True


def kernel(**inputs):
    inp = {k: np.asarray(v) for k, v in inputs.items()}
    x = inp["x"]
    p = inp["p"]
    B = x.shape[0]
    hT = [np.ascontiguousarray(x[b].T) for b in range(B)]
    if FUSED:
        pT = [np.ascontiguousarray(p[:, b].transpose(0, 2, 1)) for b in range(B)]
        outs = run_layers(hT, pT, inp, list(range(DEPTH)), True)
    else:
        for l in range(DEPTH):
            pT = [np.ascontiguousarray(p[l:l + 1, b].transpose(0, 2, 1)) for b in range(B)]
            hT = run_layers(hT, pT, inp, [l], l == DEPTH - 1)
        outs = hT
    return np.stack([np.ascontiguousarray(o.T) for o in outs]).astype(np.float32)
```
